# Optimizing a Trainium2 kernel written in Bass

```python
import jax, jax.numpy as jnp
from jax import lax
import numpy as np

D_MODEL = 1024
BATCH = 2
SEQ = 8192
DEPTH = 2

HEAD_DIM = 64
N_Q_HEADS = 8
N_KV_HEADS = 2
ATTN_WIDTH = N_Q_HEADS * HEAD_DIM
KV_WIDTH = N_KV_HEADS * HEAD_DIM
LRU_HEADS = 4
LRU_WIDTH = LRU_HEADS * HEAD_DIM
CONV_GROUPS = 4
CONV_WIDTH = CONV_GROUPS * HEAD_DIM
MIX_WIDTH = ATTN_WIDTH + LRU_WIDTH + CONV_WIDTH
IN_SPLITS = (ATTN_WIDTH, KV_WIDTH, KV_WIDTH, LRU_WIDTH, LRU_WIDTH, CONV_WIDTH, CONV_WIDTH, CONV_WIDTH)
IN_PROJ_WIDTH = sum(IN_SPLITS)
WINDOW = 128
BLOCK = 128
LRU_CONV_K = 4
SHORT_CONV_K = 3
LRU_C = 8.0
D_FF = ((8 * D_MODEL // 3 + 127) // 128) * 128
N_SUBLAYERS = 3
N_MOD = 3 * N_SUBLAYERS
EPS = 1e-6
NEG_INF = -1e30

kernel_name = "hymba_hybrid_rglru_swa_shortconv_macaron"


def rmsnorm(x, g):
    xf = x.astype(jnp.float32)
    y = xf * lax.rsqrt(jnp.mean(xf * xf, axis=-1, keepdims=True) + EPS)
    return (y * g.astype(jnp.float32)).astype(x.dtype)


def modulate(x, g, shift, scale):
    return rmsnorm(x, g) * (1.0 + scale[:, None, :]) + shift[:, None, :]


def swiglu(h, w_gu, w_down):
    gu = h @ w_gu
    g, u = jnp.split(gu, 2, axis=-1)
    return (jax.nn.silu(g) * u) @ w_down


def causal_depthwise_conv(x, w, b=None):
    K = w.shape[0]
    S = x.shape[1]
    xp = jnp.pad(x, ((0, 0), (K - 1, 0), (0, 0)))
    y = xp[:, 0:S] * w[0]
    for k in range(1, K):
        y = y + xp[:, k:k + S] * w[k]
    if b is not None:
        y = y + b
    return y


def alibi_slopes(n_heads):
    return jnp.asarray(2.0 ** (-8.0 * np.arange(1, n_heads + 1) / n_heads), dtype=jnp.float32)


def sliding_window_attention(q, k, v, sinks):
    b, s = q.shape[:2]
    nb = s // BLOCK
    G = N_Q_HEADS // N_KV_HEADS
    qb = q.reshape(b, nb, BLOCK, N_KV_HEADS, G, HEAD_DIM)

    def band(t):
        tb = t.reshape(b, nb, BLOCK, N_KV_HEADS, HEAD_DIM)
        prev = jnp.pad(tb[:, :-1], ((0, 0), (1, 0), (0, 0), (0, 0), (0, 0)))
        return jnp.concatenate([prev, tb], axis=2)

    kb, vb = band(k), band(v)
    scores = jnp.einsum('bnqkgd,bnskd->bnkgqs', qb, kb).astype(jnp.float32) * (HEAD_DIM ** -0.5)
    qi = jnp.arange(BLOCK)[:, None]
    kj = jnp.arange(2 * BLOCK)[None, :]
    dist = qi + BLOCK - kj
    in_window = (dist >= 0) & (dist < WINDOW)
    blk = jnp.arange(nb)[:, None, None]
    valid = in_window[None] & ((blk > 0) | (kj[None] >= BLOCK))
    slopes = alibi_slopes(N_Q_HEADS).reshape(N_KV_HEADS, G)
    bias = -slopes[:, :, None, None] * dist.astype(jnp.float32)
    scores = jnp.where(valid[None, :, None, None], scores + bias[None, None], NEG_INF)
    sink = sinks.astype(jnp.float32).reshape(N_KV_HEADS, G)[None, None, :, :, None, None]
    m = jnp.maximum(jnp.max(scores, axis=-1, keepdims=True), sink)
    p = jnp.exp(scores - m)
    denom = jnp.sum(p, axis=-1, keepdims=True) + jnp.exp(sink - m)
    out = jnp.einsum('bnkgqs,bnskd->bnqkgd', (p / denom).astype(v.dtype), vb)
    return out.reshape(b, s, ATTN_WIDTH)


def rg_lru(x, gate_a_w, gate_a_b, gate_x_w, gate_x_b, lam):
    b, s, _ = x.shape
    xh = x.reshape(b, s, LRU_HEADS, HEAD_DIM)
    r = jax.nn.sigmoid(jnp.einsum('bshd,hde->bshe', xh, gate_a_w).reshape(b, s, LRU_WIDTH) + gate_a_b)
    i = jax.nn.sigmoid(jnp.einsum('bshd,hde->bshe', xh, gate_x_w).reshape(b, s, LRU_WIDTH) + gate_x_b)
    log_a = -LRU_C * r.astype(jnp.float32) * jax.nn.softplus(-lam.astype(jnp.float32))
    a = jnp.exp(log_a)
    mult = jnp.sqrt(-jnp.expm1(2.0 * log_a))
    first = (jnp.arange(s) == 0)[None, :, None]
    mult = jnp.where(first, 1.0, mult)
    u = mult * (i * x).astype(jnp.float32)

    def combine(left, right):
        a1, b1 = left
        a2, b2 = right
        return a1 * a2, a2 * b1 + b2

    _, h = lax.associative_scan(combine, (a, u), axis=1)
    return h.astype(x.dtype)


def hybrid_mixer(h, w_in, w_out, sinks, lru_conv_w, lru_conv_b, lru_gate_a_w, lru_gate_a_b,
                 lru_gate_x_w, lru_gate_x_b, lru_lambda, sc_conv_w):
    proj = h @ w_in
    offs = np.cumsum(IN_SPLITS)[:-1].tolist()
    q, k, v, lx, lg, sb, sc, sx = jnp.split(proj, offs, axis=-1)
    y_attn = sliding_window_attention(q, k, v, sinks)
    lx = causal_depthwise_conv(lx, lru_conv_w, lru_conv_b)
    y_lru = jax.nn.gelu(lg) * rg_lru(lx, lru_gate_a_w, lru_gate_a_b, lru_gate_x_w, lru_gate_x_b, lru_lambda)
    y_sc = sb * causal_depthwise_conv(sc * sx, sc_conv_w)
    y = jnp.concatenate([y_attn, y_lru, y_sc], axis=-1)
    return y @ w_out


def setup_inputs(seed: int = 0) -> dict:
    key = jax.random.key(seed)
    ks = jax.random.split(key, 24)
    f32 = jnp.float32

    def nrm(k, shape, scale):
        return jax.random.normal(k, shape, f32) * scale

    u = jax.random.uniform(ks[17], (DEPTH, LRU_WIDTH), f32, 0.9, 0.999)
    a0 = u ** (1.0 / LRU_C)
    lru_lambda = jnp.log(a0) - jnp.log1p(-a0)
    return {
        "x": nrm(ks[0], (BATCH, SEQ, D_MODEL), 1.0),
        "c": nrm(ks[1], (BATCH, D_MODEL), 1.0),
        "w_mod": nrm(ks[2], (DEPTH, D_MODEL, N_MOD * D_MODEL), D_MODEL ** -0.5),
        "b_mod": nrm(ks[3], (DEPTH, N_MOD * D_MODEL), 0.02),
        "g_norm": 1.0 + nrm(ks[4], (DEPTH, N_SUBLAYERS, D_MODEL), 0.02),
        "w_ffn1_gu": nrm(ks[5], (DEPTH, D_MODEL, 2 * D_FF), D_MODEL ** -0.5),
        "w_ffn1_down": nrm(ks[6], (DEPTH, D_FF, D_MODEL), D_FF ** -0.5),
        "w_ffn2_gu": nrm(ks[7], (DEPTH, D_MODEL, 2 * D_FF), D_MODEL ** -0.5),
        "w_ffn2_down": nrm(ks[8], (DEPTH, D_FF, D_MODEL), D_FF ** -0.5),
        "w_in": nrm(ks[9], (DEPTH, D_MODEL, IN_PROJ_WIDTH), D_MODEL ** -0.5),
        "w_out": nrm(ks[10], (DEPTH, MIX_WIDTH, D_MODEL), MIX_WIDTH ** -0.5),
        "attn_sinks": nrm(ks[11], (DEPTH, N_Q_HEADS), 0.5),
        "lru_conv_w": nrm(ks[12], (DEPTH, LRU_CONV_K, LRU_WIDTH), LRU_CONV_K ** -0.5),
        "lru_conv_b": nrm(ks[13], (DEPTH, LRU_WIDTH), 0.02),
        "lru_gate_a_w": nrm(ks[14], (DEPTH, LRU_HEADS, HEAD_DIM, HEAD_DIM), HEAD_DIM ** -0.5),
        "lru_gate_a_b": nrm(ks[15], (DEPTH, LRU_WIDTH), 0.02),
        "lru_gate_x_w": nrm(ks[16], (DEPTH, LRU_HEADS, HEAD_DIM, HEAD_DIM), HEAD_DIM ** -0.5),
        "lru_gate_x_b": nrm(ks[18], (DEPTH, LRU_WIDTH), 0.02),
        "lru_lambda": lru_lambda,
        "sc_conv_w": nrm(ks[19], (DEPTH, SHORT_CONV_K, CONV_WIDTH), SHORT_CONV_K ** -0.5),
        "g_final": 1.0 + nrm(ks[20], (D_MODEL,), 0.02),
    }


def reference(x, c, w_mod, b_mod, g_norm, w_ffn1_gu, w_ffn1_down, w_ffn2_gu, w_ffn2_down,
              w_in, w_out, attn_sinks, lru_conv_w, lru_conv_b, lru_gate_a_w, lru_gate_a_b,
              lru_gate_x_w, lru_gate_x_b, lru_lambda, sc_conv_w, g_final):
    c_act = jax.nn.silu(c)
    for l in range(DEPTH):
        mod = (c_act @ w_mod[l] + b_mod[l]).reshape(c.shape[0], N_MOD, D_MODEL)
        h = modulate(x, g_norm[l, 0], mod[:, 0], mod[:, 1])
        x = x + 0.5 * mod[:, 2][:, None, :] * swiglu(h, w_ffn1_gu[l], w_ffn1_down[l])
        h = modulate(x, g_norm[l, 1], mod[:, 3], mod[:, 4])
        x = x + mod[:, 5][:, None, :] * hybrid_mixer(
            h, w_in[l], w_out[l], attn_sinks[l], lru_conv_w[l], lru_conv_b[l],
            lru_gate_a_w[l], lru_gate_a_b[l], lru_gate_x_w[l], lru_gate_x_b[l],
            lru_lambda[l], sc_conv_w[l])
        h = modulate(x, g_norm[l, 2], mod[:, 6], mod[:, 7])
        x = x + 0.5 * mod[:, 8][:, None, :] * swiglu(h, w_ffn2_gu[l], w_ffn2_down[l])
    return rmsnorm(x, g_final)
```

```python
import numpy as np
import concourse.bass as bass
import concourse.mybir as mybir
from concourse.bass_utils import run_bass_kernel_spmd

F32 = mybir.dt.float32
BF16 = mybir.dt.bfloat16
AF = mybir.ActivationFunctionType
ALU = mybir.AluOpType
AX = mybir.AxisListType

NCORES = 8
NG = 4
T = 2048
NTT = 4
D = 1024
DFF = 2816
NJ = 22
GROUPS = [(0, 8), (8, 16), (16, 22)]
DEPTH = 2
ESZ = {F32: 4, BF16: 2}

CH_Q = [0, 1, 2, 3]
CH_K = [4, 5]
CH_V = 6
CH_LX = [7, 8]
CH_LG = [9, 10]
CH_SB = [11, 12]
CH_SC = [13, 14]
CH_SX = [15, 16]
NCH_IN = 17
HS = [0, 2, 4, 6, 1, 3, 5, 7]

CP = {}
_o = 0
for _n, _w in [("cT", 8), ("gn", 48), ("gfin", 8), ("convw", 16), ("convb", 4), ("ba", 4), ("bx", 4),
               ("lam", 4), ("scw", 12), ("sinks", 16), ("selE1", 4), ("selE2", 8), ("omselE2", 8),
               ("f", 1), ("omf", 1), ("negbig", 1), ("eps", 1), ("one", 1), ("zero", 1),
               ("identf", 128)]:
    CP[_n] = (_o, _o + _w)
    _o += _w
NCP = _o
CB = {}
_o = 0
for _n, _w in [("ident", 128), ("ones", 128), ("gbd", 1024), ("bias", 2048)]:
    CB[_n] = (_o, _o + _w)
    _o += _w
NCB = _o


class V:
    __slots__ = ("ap", "reg")

    def __init__(self, ap, reg):
        self.ap = ap
        self.reg = reg

    def w(self, ap):
        return V(ap, self.reg)


class Buf:
    def __init__(self, handle, shape, dtype, space, base):
        self.h = handle
        self.shape = list(shape)
        self.dtype = dtype
        self.space = space
        self.base = base
        self.esz = ESZ[dtype]
        st = [1] * len(shape)
        for i in range(len(shape) - 2, 0, -1):
            st[i] = st[i + 1] * shape[i + 1]
        self.strides = st

    def __getitem__(self, idx):
        if not isinstance(idx, tuple):
            idx = (idx,)
        idx = list(idx) + [slice(None)] * (len(self.shape) - len(idx))
        rng = []
        for d, i in enumerate(idx):
            if isinstance(i, int):
                rng.append((i, i + 1))
            else:
                lo = 0 if i.start is None else i.start
                hi = self.shape[d] if i.stop is None else i.stop
                rng.append((lo, hi))
        p0, p1 = rng[0]
        fr = rng[1:]
        sh = self.shape[1:]
        st = self.strides[1:]
        k = len(fr) - 1
        while k > 0 and fr[k] == (0, sh[k]):
            k -= 1
        if len(fr) == 0:
            ivs = [(self.base, self.base + self.esz)]
        else:
            blk0 = fr[k][0] * st[k]
            blk1 = fr[k][1] * st[k]
            outer = [0]
            for d in range(k):
                outer = [o + i * st[d] for o in outer for i in range(fr[d][0], fr[d][1])]
            if len(outer) > 64:
                lo = min(outer) + blk0
                hi = max(outer) + blk1
                ivs = [(self.base + lo * self.esz, self.base + hi * self.esz)]
            else:
                ivs = [(self.base + (o + blk0) * self.esz, self.base + (o + blk1) * self.esz) for o in outer]
        return V(self.h[tuple(idx)], (self.space, p0, p1, ivs))


class Prog:
    ENGS = ("pe", "act", "dve", "pool", "sp")

    def __init__(self):
        self.nc = bass.Bass("TRN2", target_bir_lowering=False)
        self.ops = {e: [] for e in self.ENGS}
        self.recs = {}
        self.slot_val = {}
        self.slot_inc = {}
        self.sb_lo = ((self.nc.sbuf_base + 63) // 64) * 64
        self.sb_hi = self.nc.sbuf_top
        self.sb_ptr = self.sb_lo
        self.nalloc = 0
        self.ps = []
        for b in range(8):
            h = self.nc.alloc_psum_tensor(f"psb{b}", [128, 512], F32)
            self.ps.append(h)
        self.drams = {}

    def sb(self, name, shape, dtype, at=None):
        nbytes = int(np.prod(shape[1:])) * ESZ[dtype]
        nbytes = ((nbytes + 63) // 64) * 64
        if at is None:
            at = self.sb_ptr
            self.sb_ptr += nbytes
        assert at % 32 == 0
        assert at + nbytes <= self.sb_hi, (name, at, nbytes, self.sb_hi)
        self.nalloc += 1
        h = self.nc.alloc_sbuf_tensor_at(f"{name}_{self.nalloc}", list(shape), dtype, offset=at)
        b = Buf(h, shape, dtype, "sb", at)
        b.nbytes = nbytes
        return b

    def psum(self, bank, shape, dtype=F32):
        h = self.ps[bank]
        if dtype == BF16:
            h = h.bitcast(BF16)
            full = [128, 1024]
        else:
            full = [128, 512]
        if list(shape) != full:
            names = " ".join(f"d{i}" for i in range(len(shape) - 1))
            kw = {f"d{i}": shape[i + 1] for i in range(len(shape) - 1)}
            need = int(np.prod(shape[1:]))
            if need != full[1]:
                h = h[:, 0:need]
            h = h.rearrange(f"p ({names}) -> p {names}", **kw)
        return Buf(h, shape, dtype, "ps", bank * 2048)

    def dram(self, name, shape, dtype, kind=None):
        if kind is None:
            h = self.nc.dram_tensor(name, list(shape), dtype)
        else:
            h = self.nc.dram_tensor(name, list(shape), dtype, kind=kind)
        self.drams[name] = h
        return h

    def dv(self, name, ap, track=True):
        if not track:
            return V(ap, None)
        return V(ap, (("dram", name), 0, 1, [(0, 1)]))

    def _buckets(self, reg):
        space, p0, p1, ivs = reg
        out = []
        for (b0, b1) in ivs:
            for k in range(b0 // 1024, (b1 - 1) // 1024 + 1):
                out.append((space, k))
        return out

    @staticmethod
    def _overlap(r1, r2):
        if r1[1] >= r2[2] or r2[1] >= r1[2]:
            return False
        for (a0, a1) in r1[3]:
            for (b0, b1) in r2[3]:
                if a0 < b1 and b0 < a1:
                    return True
        return False

    @staticmethod
    def _covers(big, small):
        if big[1] > small[1] or big[2] < small[2]:
            return False
        for (s0, s1) in small[3]:
            ok = False
            for (b0, b1) in big[3]:
                if b0 <= s0 and s1 <= b1:
                    ok = True
                    break
            if not ok:
                return False
        return True

    def _query(self, reg):
        seen = set()
        res = []
        for bk in self._buckets(reg):
            for rec in self.recs.get(bk, ()):
                if id(rec) in seen or rec["dead"]:
                    continue
                seen.add(id(rec))
                if self._overlap(rec["reg"], reg):
                    res.append(rec)
        return res

    def _insert(self, rec):
        for bk in self._buckets(rec["reg"]):
            self.recs.setdefault(bk, []).append(rec)

    def _gc(self, reg):
        for bk in self._buckets(reg):
            lst = self.recs.get(bk)
            if lst and len(lst) > 24:
                self.recs[bk] = [r for r in lst if not r["dead"]]

    def op(self, eng, fn, reads=(), writes=(), dma=None, inc=16):
        idx = len(self.ops[eng])
        if dma is not None:
            self.slot_inc.setdefault(dma, inc)
            assert self.slot_inc[dma] == inc
            val = self.slot_val.get(dma, 0) + inc
            self.slot_val[dma] = val
            tok = ("d", dma, val)
            tok_eng = "dma:" + dma
        else:
            tok = ("e", eng, idx)
            tok_eng = eng
        deps = set()
        for v in reads:
            if v is None or v.reg is None:
                continue
            for rec in self._query(v.reg):
                if rec["kind"] == "W":
                    deps.add(rec["tok"])
        for v in writes:
            if v is None or v.reg is None:
                continue
            for rec in self._query(v.reg):
                if rec["eng"] == tok_eng and dma is None and eng == "pe":
                    if self._covers(v.reg, rec["reg"]):
                        rec["dead"] = True
                    continue
                deps.add(rec["tok"])
                if self._covers(v.reg, rec["reg"]):
                    rec["dead"] = True
        for v in reads:
            if v is None or v.reg is None:
                continue
            found = False
            for rec in self._query(v.reg):
                if rec["kind"] == "R" and rec["eng"] == tok_eng and rec["reg"] == v.reg and dma is None:
                    rec["tok"] = tok
                    found = True
                    break
            if not found:
                self._insert({"reg": v.reg, "kind": "R", "tok": tok, "eng": tok_eng, "dead": False})
        for v in writes:
            if v is None or v.reg is None:
                continue
            self._insert({"reg": v.reg, "kind": "W", "tok": tok, "eng": tok_eng, "dead": False})
            self._gc(v.reg)
        deps.discard(tok)
        self.ops[eng].append({"fn": fn, "deps": deps, "dma": dma, "inc": inc, "sig": False})
        return tok

    def finalize(self, final_waits):
        nc = self.nc
        for e in self.ENGS:
            for o in self.ops[e]:
                for t in o["deps"]:
                    if t[0] == "e":
                        self.ops[t[1]][t[2]]["sig"] = True
        counts = {}
        for e in self.ENGS:
            c = 0
            lst = []
            for o in self.ops[e]:
                if o["sig"] and o["dma"] is None:
                    c += 1
                lst.append(c)
            counts[e] = lst
        esem = {e: nc.alloc_semaphore(f"sem_{e}") for e in self.ENGS}
        dsem = {s: nc.alloc_semaphore(f"dsem_{s}") for s in self.slot_val}
        nwaits = 0
        engobj = {"pe": "tensor", "act": "scalar", "dve": "vector", "pool": "gpsimd", "sp": "sync"}

        def resolve(t):
            if t[0] == "e":
                return esem[t[1]], counts[t[1]][t[2]], ("e", t[1])
            return dsem[t[1]], t[2], ("d", t[1])

        def emit(e, engine):
            nonlocal nwaits
            waited = {}
            for i, o in enumerate(self.ops[e]):
                need = {}
                for t in o["deps"]:
                    sem, val, key = resolve(t)
                    if key == ("e", e) and t[2] >= i:
                        raise RuntimeError("forward dep")
                    if waited.get(key, 0) >= val:
                        continue
                    if need.get(key, (None, 0))[1] < val:
                        need[key] = (sem, val)
                for key, (sem, val) in need.items():
                    engine.wait_ge(sem, val)
                    waited[key] = val
                    nwaits += 1
                ins = o["fn"](engine)
                if o["dma"] is not None:
                    ins.then_inc(dsem[o["dma"]], o["inc"])
                elif o["sig"]:
                    ins.then_inc(esem[e], 1)
            if e == "sp":
                for t in final_waits:
                    sem, val, key = resolve(t)
                    engine.wait_ge(sem, val)

        with nc.Block() as block:
            @block.tensor
            def _(eng):
                emit("pe", eng)

            @block.scalar
            def _(eng):
                emit("act", eng)

            @block.vector
            def _(eng):
                emit("dve", eng)

            @block.gpsimd
            def _(eng):
                emit("pool", eng)

            @block.sync
            def _(eng):
                emit("sp", eng)
        self.stats = {e: len(self.ops[e]) for e in self.ENGS}
        self.stats["waits"] = nwaits
        return nc


def build_program(stages, fused=True):
    P = Prog()
    nc = P.nc
    P.ext_in = []
    P.ext_out = []
    _dr = {}

    def DR(name, shape=None, dtype=F32, kind=None):
        if name not in _dr:
            _dr[name] = P.dram(name, shape, dtype, kind)
            if kind == "ExternalInput":
                P.ext_in.append(name)
            elif kind == "ExternalOutput":
                P.ext_out.append(name)
        return _dr[name]

    d_xT = DR("xT", [D, T], F32, "ExternalInput")
    d_cp = DR("cpack", [128, NCP], F32, "ExternalInput")
    d_cb = DR("cpackb", [128, NCB], F32, "ExternalInput")
    d_wgu = lambda fi: DR(f"wgu{fi}", [DEPTH, NJ, 128, 8, 256], F32, "ExternalInput")
    d_wdn = lambda fi: DR(f"wdn{fi}", [DEPTH, DFF, D], F32, "ExternalInput")
    d_win = lambda: DR("win", [DEPTH, NCH_IN, 128, 8, 128], F32, "ExternalInput")
    d_wout = lambda: DR("wout", [DEPTH, 8, 128, 8, 128], F32, "ExternalInput")
    out_tokens = []
    RG = [[0, 1, 2, 3], [4, 5, 6, 7]]

    X = P.sb("X", [128, 8, T], F32)
    H = P.sb("H", [128, 8, T], BF16)
    CPK = P.sb("CPK", [128, NCP], F32)
    CBK = P.sb("CBK", [128, NCB], BF16)
    MT = P.sb("MT", [128, DEPTH, 72], F32)
    AM = P.sb("AM", [128, DEPTH, 3, 8], F32)
    GT = P.sb("GT", [128, DEPTH, 3, 8], F32)
    LC = P.sb("LC", [128, DEPTH, 2, 2], F32)
    CACT = P.sb("CACT", [128, 8, 1], BF16)
    HSEL = P.sb("HSEL", [128, 400], F32)
    E2ST = P.sb("E2ST", [128, 8], F32)
    HIN = P.sb("HIN", [128, 2], F32)
    SMALL = P.sb("SMALL", [128, 128], F32)
    arena0 = P.sb_ptr
    arena_sz = P.sb_hi - arena0

    def cp(name, lo=None, hi=None):
        a, b = CP[name]
        if lo is None:
            return CPK[:, a:b]
        return CPK[:, a + lo:a + hi]

    def cb(name, lo=None, hi=None):
        a, b = CB[name]
        if lo is None:
            return CBK[:, a:b]
        return CBK[:, a + lo:a + hi]

    class Arena:
        def __init__(self, base):
            self.p = base

        def sb(self, name, shape, dtype):
            b = P.sb(name, shape, dtype, at=self.p)
            self.p += b.nbytes
            assert self.p <= P.sb_hi, (name, self.p - P.sb_hi)
            return b

    fa = Arena(arena0)
    ACTB = fa.sb("ACTB", [128, 8, T], BF16)
    WGU = [fa.sb(f"WGU{i}", [128, 8, 256], BF16) for i in range(3)]
    WD = fa.sb("WD", [128, 8, D], BF16)
    SG = [fa.sb(f"SG{i}", [128, 512], F32) for i in range(2)]
    XSQ = [fa.sb(f"XSQ{i}", [128, 512], BF16) for i in range(8)]
    TMPN = [fa.sb(f"TMPN{i}", [128, 512], F32) for i in range(2)]
    LNT = fa.sb("LNT", [128, 512], F32)
    ffn_end = fa.p
    pa = Arena(arena0)
    WM = pa.sb("WM", [128, 8, 2304], BF16)
    BM = pa.sb("BM", [1, DEPTH, 2304], F32)
    MP = pa.sb("MP", [1, DEPTH, 2304], F32)
    MR = [pa.sb(f"MR{i}", [72, 128], F32) for i in range(2)]
    ma = Arena(arena0)
    QT = ma.sb("QT", [128, 4, T], BF16)
    YL = ma.sb("YL", [128, 2, T], BF16)
    GP = ma.sb("GP", [128, 2, T], BF16)
    YS = ma.sb("YS", [128, 2, T], BF16)
    WIN = [ma.sb(f"WIN{i}", [128, 8, 128], BF16) for i in range(3)]
    WOUT = [ma.sb(f"WOUT{i}", [128, 8, 128], BF16) for i in range(2)]
    HST = ma.sb("HST", [128, 400], F32)
    E2ALL = ma.sb("E2ALL", [128, NG, 8], F32)
    E2PM = ma.sb("E2PM", [128, NG, 2], F32)
    E2HM = ma.sb("E2HM", [128, NG, 2], F32)
    phase0 = ma.p
    wa = Arena(phase0)
    WPRE = [wa.sb(f"WPRE{i}", [128, 8, 128], BF16) for i in range(9)]
    aa = Arena(phase0)
    KT = aa.sb("KT", [128, 2, 128 + T], BF16)
    VV = aa.sb("VV", [128, 17, 128], BF16)
    SSB = aa.sb("SSB", [128, 8, 256], F32)
    PEXP = aa.sb("PEXP", [128, 8, 256], BF16)
    PEXP1 = aa.sb("PEXP1", [128, 8, 256], BF16)
    PTS = aa.sb("PTS", [128, 16, 128], BF16)
    YTOK = aa.sb("YTOK", [128, 512], BF16)
    HALL = aa.sb("HALL", [128, NG, 400], F32)
    SSB1 = P.sb("SSB1", [128, 8, 256], F32, at=HALL.base)
    aa.p = max(aa.p, HALL.base + SSB1.nbytes)
    assert aa.p <= P.sb_hi
    la = Arena(phase0)
    LXB = la.sb("LXB", [128, 3 + T], F32)
    LXC = [la.sb(f"LXC{i}", [128, 512], F32) for i in range(2)]
    LXCB = [la.sb(f"LXCB{i}", [128, 512], BF16) for i in range(2)]
    RR = la.sb("RR", [128, 512], F32)
    II = la.sb("II", [128, 512], F32)
    AA = la.sb("AA", [128, 512], F32)
    A2 = la.sb("A2", [128, 512], F32)
    MM = la.sb("MM", [128, 512], F32)
    UU = la.sb("UU", [128, 512], F32)
    HL = [la.sb(f"HL{i}", [128, 512], F32) for i in range(2)]
    PP = [la.sb(f"PP{i}", [128, 512], F32) for i in range(2)]
    GG = la.sb("GG", [128, 512], F32)
    ZERO = la.sb("ZERO", [128, 512], BF16)
    TMPC = la.sb("TMPC", [128, 512], F32)
    sa = Arena(phase0)
    ZB = sa.sb("ZB", [128, 2 + T], F32)
    ZC = [sa.sb(f"ZC{i}", [128, 512], F32) for i in range(2)]
    TMPZ = sa.sb("TMPZ", [128, 512], F32)

    ident = cb("ident")
    onesb = cb("ones")

    PS = lambda b: P.psum(b, [128, 512], F32)

    def mm(out, lhsT, rhs, start, stop):
        P.op("pe", lambda e: e.matmul(out.ap, lhsT.ap, rhs.ap, start=start, stop=stop),
             reads=[lhsT, rhs], writes=[out])

    def tr(out, in_, idv):
        P.op("pe", lambda e: e.transpose(out.ap, in_.ap, idv.ap), reads=[in_, idv], writes=[out])

    def act(out, in_, func, bias=None, scale=None, accum=None):
        rd = [in_]
        kw = {}
        if bias is not None:
            if isinstance(bias, V):
                rd.append(bias)
                kw["bias"] = bias.ap
            else:
                kw["bias"] = bias
        if scale is not None:
            if isinstance(scale, V):
                rd.append(scale)
                kw["scale"] = scale.ap
            else:
                kw["scale"] = scale
        wr = [out]
        if accum is not None:
            kw["accum_out"] = accum.ap
            wr.append(accum)
        P.op("act", lambda e: e.activation(out.ap, in_.ap, func, **kw), reads=rd, writes=wr)

    def tt(out, in0, in1, op, eng="dve"):
        P.op(eng, lambda e: e.tensor_tensor(out.ap, in0.ap, in1.ap, op), reads=[in0, in1], writes=[out])

    def ts(out, in0, s1, op0, s2=None, op1=None, eng="dve"):
        rd = [in0]
        a1 = s1.ap if isinstance(s1, V) else s1
        a2 = s2.ap if isinstance(s2, V) else s2
        if isinstance(s1, V):
            rd.append(s1)
        if isinstance(s2, V):
            rd.append(s2)
        if op1 is None:
            P.op(eng, lambda e: e.tensor_scalar(out.ap, in0.ap, a1, None, op0), reads=rd, writes=[out])
        else:
            P.op(eng, lambda e: e.tensor_scalar(out.ap, in0.ap, a1, a2, op0, op1), reads=rd, writes=[out])

    def stt(out, in0, sc, in1, op0, op1):
        rd = [in0, in1]
        a = sc.ap if isinstance(sc, V) else sc
        if isinstance(sc, V):
            rd.append(sc)
        P.op("dve", lambda e: e.scalar_tensor_tensor(out.ap, in0.ap, a, in1.ap, op0, op1), reads=rd, writes=[out])

    def cpy(out, in_, eng="dve"):
        P.op(eng, lambda e: e.tensor_copy(out.ap, in_.ap), reads=[in_], writes=[out])

    def dma(eng, out, in_, slot):
        return P.op(eng, lambda e: e.dma_start(out=out.ap, in_=in_.ap), reads=[in_], writes=[out], dma=slot)

    def collective(in_name, out_name, slot):
        hin, hout = P.drams[in_name], P.drams[out_name]
        vin = P.dv(in_name, None)
        vout = P.dv(out_name, None)
        return P.op("pool", lambda e: e.collective_compute("AllGather", ALU.bypass, replica_groups=RG,
                                                          ins=[hin.ap().opt()], outs=[hout.ap().opt()]),
                    reads=[vin], writes=[vout], dma=slot, inc=1)

    def pro_common():
        for dc in range(8):
            dma("sp", X[:, dc, :], P.dv("xT", d_xT[dc * 128:(dc + 1) * 128, :], False), f"xld{dc}")
        dma("sp", CPK[:, :], P.dv("cpack", d_cp[:, :], False), "cpk")
        for i, (a, b) in enumerate([(0, 1280), (1280, 2304), (2304, NCB)]):
            dma("pool", CBK[:, a:b], P.dv("cpackb", d_cb[:, a:b], False), f"cbk{i}")

    def pro_modparts():
        d_wmod = DR("wmod", [DEPTH, 128, 8, 2304], F32, "ExternalInput")
        d_bmod = DR("bmod", [1, DEPTH, 2304], F32, "ExternalInput")
        dma("sp", BM[:, :, :], P.dv("bmod", d_bmod[:, :, :], False), "bmld")
        act(CACT[:, :, :].w(CACT[:, :, :].ap.rearrange("p a b -> p (a b)")), cp("cT"), AF.Silu)
        for l in range(DEPTH):
            for kc in range(8):
                for hf in range(2):
                    dma("pool", WM[:, kc, hf * 1152:(hf + 1) * 1152],
                        P.dv("wmod", d_wmod[l, :, kc, hf * 1152:(hf + 1) * 1152], False), f"wm{kc}_{hf}")
            for n6 in range(6):
                pb = P.psum(n6 % 4, [128, 512], F32)
                for kc in range(8):
                    mm(pb[0:1, 0:384], CACT[:, kc, :], WM[:, kc, n6 * 384:(n6 + 1) * 384], kc == 0, kc == 7)
                tt(MP[0:1, l, n6 * 384:(n6 + 1) * 384], pb[0:1, 0:384], BM[0:1, l, n6 * 384:(n6 + 1) * 384], ALU.add)
        if fused:
            d_modpart = DR("modpart", [36, 128], F32)
            pname = "modpart"
        else:
            d_modpart = DR("o_modpart", [36, 128], F32, "ExternalOutput")
            pname = "o_modpart"
        dst_ap = d_modpart.ap().rearrange("(q l) p -> l q p", l=DEPTH)
        for l in range(DEPTH):
            src = MP[0:1, l, :]
            t = dma("sp", P.dv(pname, dst_ap[l:l + 1]), src.w(src.ap.rearrange("b (q p) -> b q p", p=128)), f"mpst{l}")
            if not fused:
                out_tokens.append(t)
        if fused:
            DR("modall", [36 * NG, 128], F32)
            collective("modpart", "modall", "ccm")

    def pro_modfinish():
        if fused:
            d_modall = DR("modall", [36 * NG, 128], F32)
            mname, trk = "modall", True
        else:
            d_modall = DR("i_modall", [36 * NG, 128], F32, "ExternalInput")
            mname, trk = "i_modall", False
        gall = d_modall.ap().rearrange("(j l) p -> l j p", l=DEPTH)
        for l in range(DEPTH):
            dma("sp", MR[l][:, :], P.dv(mname, gall[l], trk), f"mrl{l}")
        identf = cp("identf")
        for l in range(DEPTH):
            pb = P.psum(4 + l, [128, 512], F32)
            tr(pb[:, 0:72], MR[l][:, :], CPK[0:72, CP["identf"][0]:CP["identf"][0] + 72])
            cpy(MT[:, l, :], pb[:, 0:72])
            for s in range(3):
                gn = cp("gn", (l * 3 + s) * 8, (l * 3 + s) * 8 + 8)
                stt(AM[:, l, s, :], MT[:, l, (3 * s + 1) * 8:(3 * s + 1) * 8 + 8], 1.0, gn, ALU.add, ALU.mult)
                ts(GT[:, l, s, :], MT[:, l, (3 * s + 2) * 8:(3 * s + 2) * 8 + 8], 1.0 if s == 1 else 0.5, ALU.mult)
        lam = cp("lam")
        ee = SMALL[:, 0:4]
        pq = SMALL[:, 4:8]
        t2 = SMALL[:, 8:12]
        act(ee, lam, AF.Exp, scale=-1.0)
        ts(pq, ee, -0.2, ALU.mult, 0.25, ALU.add)
        for cst in (1.0 / 3.0, 0.5, 1.0):
            tt(t2, ee, pq, ALU.mult)
            ts(pq, t2, -1.0, ALU.mult, cst, ALU.add)
        tt(t2, ee, pq, ALU.mult)
        o0 = LC[:, :, :, 0:1]
        o1 = LC[:, :, :, 1:2]
        ts(o0.w(o0.ap.rearrange("p l c o -> p (l c o)")), t2, -8.0, ALU.mult)
        ts(o1.w(o1.ap.rearrange("p l c o -> p (l c o)")), t2, -16.0, ALU.mult)

    def norm_sq(t4):
        c0, c1 = t4 * 512, (t4 + 1) * 512
        for dc in range(8):
            if dc % 8 in (2, 5, 7):
                act(XSQ[dc][:, :], X[:, dc, c0:c1], AF.Square)
            else:
                tt(XSQ[dc][:, :], X[:, dc, c0:c1], X[:, dc, c0:c1], ALU.mult, eng="pool")

    def norm_rest(l, s, t4, final=False):
        c0, c1 = t4 * 512, (t4 + 1) * 512
        pst = PS(6)
        prs = PS(7)
        for dc in range(8):
            mm(pst[:, :], onesb, XSQ[dc][:, :], dc == 0, dc == 7)
        act(LNT[:, :], pst[:, :], AF.Ln, bias=cp("eps"), scale=1.0 / D)
        act(prs[:, :], LNT[:, :], AF.Exp, scale=-0.5)
        for dc in range(8):
            if final:
                o = TMPN[dc % 2]
                stt(o[:, :], X[:, dc, c0:c1], cp("gfin", dc, dc + 1), prs[:, :], ALU.mult, ALU.mult)
                dma("sp", P.dv("outT", DR("outT", [D, T], F32, "ExternalOutput")[dc * 128:(dc + 1) * 128, c0:c1]), o[:, :], f"ost{dc % 2}")
            else:
                tt(TMPN[dc % 2][:, :], X[:, dc, c0:c1], prs[:, :], ALU.mult)
                act(H[:, dc, c0:c1], TMPN[dc % 2][:, :], AF.Identity,
                    bias=MT[:, l, 3 * s * 8 + dc:3 * s * 8 + dc + 1], scale=AM[:, l, s, dc:dc + 1])

    def norm_tile(l, s, t4, final=False):
        norm_sq(t4)
        norm_rest(l, s, t4, final)

    def ffn(l, s, pre_normed=False, next_norm=None):
        fi = 0 if s == 0 else 1
        issued = set()

        def load_wgu(j):
            if j in issued or j >= NJ:
                return
            issued.add(j)
            dma("pool", WGU[j % 3][:, :, :], P.dv("wgu", d_wgu(fi)[l, j], False), f"wgu{j % 3}")

        def load_wd(j0, j1):
            for jj in range(j1 - j0):
                j = j0 + jj
                dma("pool", WD[:, jj, :], P.dv("wdn", d_wdn(fi)[l, j * 128:(j + 1) * 128, :], False), f"wd{jj}")

        for j in range(3):
            load_wgu(j)
        load_wd(*GROUPS[0])
        if not pre_normed:
            for t4 in range(NTT):
                norm_tile(l, s, t4)
        cnt = 0
        for gi, (j0, j1) in enumerate(GROUPS):
            G = j1 - j0
            if gi > 0:
                load_wd(j0, j1)
            for jj in range(G):
                j = j0 + jj
                load_wgu(j)
                wt = WGU[j % 3]
                for t4 in range(NTT):
                    c0, c1 = t4 * 512, (t4 + 1) * 512
                    pg = PS((cnt % 2) * 2)
                    pu = PS((cnt % 2) * 2 + 1)
                    for kc in range(8):
                        mm(pg[:, :], wt[:, kc, 0:128], H[:, kc, c0:c1], kc == 0, kc == 7)
                    for kc in range(8):
                        mm(pu[:, :], wt[:, kc, 128:256], H[:, kc, c0:c1], kc == 0, kc == 7)
                    sg = SG[cnt % 2]
                    act(sg[:, :], pg[:, :], AF.Silu)
                    tt(ACTB[:, jj, c0:c1], pu[:, :], sg[:, :], ALU.mult)
                    cnt += 1
            for j in range(j1, j1 + 3):
                load_wgu(j)
            oc = 0
            for t4 in range(NTT):
                c0, c1 = t4 * 512, (t4 + 1) * 512
                for dc in range(8):
                    po = PS(4 + oc % 2)
                    oc += 1
                    for jj in range(G):
                        mm(po[:, :], WD[:, jj, dc * 128:(dc + 1) * 128], ACTB[:, jj, c0:c1], jj == 0, jj == G - 1)
                    stt(X[:, dc, c0:c1], po[:, :], GT[:, l, s, dc:dc + 1], X[:, dc, c0:c1], ALU.mult, ALU.add)
                if next_norm is not None and gi == len(GROUPS) - 1:
                    if t4 >= 1:
                        norm_rest(next_norm[0], next_norm[1], t4 - 1, next_norm[2])
                    norm_sq(t4)
            if next_norm is not None and gi == len(GROUPS) - 1:
                norm_rest(next_norm[0], next_norm[1], NTT - 1, next_norm[2])

    def load_win(l, ch, i):
        dma("pool", WIN[i % 3][:, :, :], P.dv("win", d_win()[l, ch], False), f"win{i % 3}")
        return WIN[i % 3]

    wi = [0]
    pc = [0]

    def nps():
        b = pc[0] % 4
        pc[0] += 1
        return PS(b)

    def mix_a(l, pre_normed=False):
        def nextw(ch):
            w = load_win(l, ch, wi[0])
            wi[0] += 1
            return w

        pre_ch = [CH_K[0], CH_K[1], CH_V, CH_LX[0], CH_LX[1], CH_SC[0], CH_SX[0], CH_SC[1], CH_SX[1]]
        for i, ch in enumerate(pre_ch):
            dma("pool", WPRE[i][:, :, :], P.dv("win", d_win()[l, ch], False), f"wpre{i}")
        wpre = WPRE
        if not pre_normed:
            for t4 in range(NTT):
                norm_tile(l, 1, t4)
        for c in range(2):
            w = wpre[c]
            pb = nps()
            for kc in range(8):
                mm(pb[:, 0:128], w[:, kc, :], H[:, kc, T - 128:T], kc == 0, kc == 7)
            act(HST[:, c * 128:(c + 1) * 128], pb[:, 0:128], AF.Copy)
        w = wpre[2]
        pb = nps()
        for kc in range(8):
            mm(pb[:, 0:128], H[:, kc, T - 128:T], w[:, kc, :], kc == 0, kc == 7)
        act(HST[:, 256:384], pb[:, 0:128], AF.Copy)
        for c in range(2):
            w = wpre[3 + c]
            pb = nps()
            for kc in range(8):
                mm(pb[:, 0:4], w[:, kc, :], H[:, kc, T - 4:T], kc == 0, kc == 7)
            act(HST[:, 384 + 3 * c:387 + 3 * c], pb[:, 1:4], AF.Copy)
        for c in range(2):
            w = wpre[5 + 2 * c]
            pb = nps()
            for kc in range(8):
                mm(pb[:, 0:2], w[:, kc, :], H[:, kc, T - 2:T], kc == 0, kc == 7)
            act(SMALL[:, 60:62], pb[:, 0:2], AF.Copy)
            w = wpre[6 + 2 * c]
            pb2 = nps()
            for kc in range(8):
                mm(pb2[:, 0:2], w[:, kc, :], H[:, kc, T - 2:T], kc == 0, kc == 7)
            tt(HST[:, 390 + 2 * c:392 + 2 * c], pb2[:, 0:2], SMALL[:, 60:62], ALU.mult)
        P.op("dve", lambda e: e.memset(HST[:, 394:400].ap, 0.0), writes=[HST[:, 394:400]])
        if fused:
            d1 = DR(f"e1in{l}", [128, 400], F32)
            DR(f"e1out{l}", [128 * NG, 400], F32)
            dma("sp", P.dv(f"e1in{l}", d1[:, :]), HST[:, :], "e1st")
            collective(f"e1in{l}", f"e1out{l}", "cc1")
        else:
            d1 = DR("o_e1", [128, 400], F32, "ExternalOutput")
            out_tokens.append(dma("sp", P.dv("o_e1", d1[:, :]), HST[:, :], "e1st"))

    def mix_b(l):
        def nextw(ch):
            w = load_win(l, ch, wi[0])
            wi[0] += 1
            return w

        if fused:
            d1o = DR(f"e1out{l}", [128 * NG, 400], F32)
            dma("sp", HALL[:, :, :], P.dv(f"e1out{l}", d1o.ap().rearrange("(r p) f -> p r f", p=128)), "e1ld")
        else:
            for t4 in range(NTT):
                norm_tile(l, 1, t4)
            d1o = DR("i_e1all", [128 * NG, 400], F32, "ExternalInput")
            dma("sp", HALL[:, :, :], P.dv("i_e1all", d1o.ap().rearrange("(r p) f -> p r f", p=128), False), "e1ld")

        for c in range(4):
            w = nextw(CH_Q[c])
            for t4 in range(NTT):
                c0, c1 = t4 * 512, (t4 + 1) * 512
                pb = nps()
                for kc in range(8):
                    mm(pb[:, :], w[:, kc, :], H[:, kc, c0:c1], kc == 0, kc == 7)
                act(QT[:, c, c0:c1], pb[:, :], AF.Copy)
        for c in range(2):
            w = nextw(CH_K[c])
            for t4 in range(NTT):
                c0, c1 = t4 * 512, (t4 + 1) * 512
                pb = nps()
                for kc in range(8):
                    mm(pb[:, :], w[:, kc, :], H[:, kc, c0:c1], kc == 0, kc == 7)
                cpy(KT[:, c, 128 + c0:128 + c1], pb[:, :])
        w = nextw(CH_V)
        for t4 in range(NTT):
            pb = nps()
            for q in range(4):
                blk = t4 * 4 + q
                for kc in range(8):
                    mm(pb[:, q * 128:(q + 1) * 128], H[:, kc, blk * 128:(blk + 1) * 128], w[:, kc, :], kc == 0, kc == 7)
            o = VV[:, 1 + t4 * 4:5 + t4 * 4, :]
            cpy(o.w(o.ap.rearrange("p a b -> p (a b)")), pb[:, :])
        ts(HSEL[:, :], HALL[:, 0, :], cp("selE1", 0, 1), ALU.mult)
        for r in range(1, NG):
            stt(HSEL[:, :], HALL[:, r, :], cp("selE1", r, r + 1), HSEL[:, :], ALU.mult, ALU.add)
        for c in range(2):
            cpy(KT[:, c, 0:128], HSEL[:, c * 128:(c + 1) * 128])
        cpy(VV[:, 0, :], HSEL[:, 256:384])

        sink = cp("sinks", l * 8, l * 8 + 8)
        ptb = [P.psum(4, [128, 8, 128], BF16), P.psum(5, [128, 8, 128], BF16)]
        PEX = [PEXP, PEXP1]

        def sm(st, k):
            return SMALL[:, 64 * st + 8 * k:64 * st + 8 * k + 8]

        def a_scores(n):
            q0, q1 = n * 128, (n + 1) * 128
            for sl in range(8):
                h = HS[sl]
                pr = (h % 2) * 64
                pb = PS(sl // 2)
                mm(pb[:, (sl % 2) * 256:(sl % 2) * 256 + 256], QT[pr:pr + 64, h // 2, q0:q1],
                   KT[pr:pr + 64, h // 4, q0:q0 + 256], True, True)

        SS = [SSB, SSB1]

        def a_softmax1(n):
            st = n % 2
            SSB = SS[st]
            MX, NMX, D8 = sm(st, 0), sm(st, 1), sm(st, 3)
            for b in range(4):
                o = SSB[:, 2 * b:2 * b + 2, :]
                bi = CBK[:, CB["bias"][0] + b * 512:CB["bias"][0] + (b + 1) * 512]
                stt(o.w(o.ap.rearrange("p a b -> p (a b)")), PS(b)[:, :], 0.125, bi, ALU.mult, ALU.add)
            if n == 0:
                o = SSB[:, :, 0:128]
                ts(o, o, cp("negbig"), ALU.add)
            P.op("dve", lambda e: e.tensor_reduce(MX.ap, SSB[:, :, :].ap, AX.X, ALU.max), reads=[SSB[:, :, :]], writes=[MX])
            tt(MX, MX, sink, ALU.max)
            ts(NMX, MX, -1.0, ALU.mult)
            tt(D8, sink, NMX, ALU.add)

        def a_exp(n):
            st = n % 2
            o = 64 * st
            for h in range(8):
                act(PEX[st][:, h, :], SS[st][:, h, :], AF.Exp, bias=SMALL[:, o + 8 + h:o + 9 + h], scale=1.0,
                    accum=SMALL[:, o + 16 + h:o + 17 + h])
            act(sm(st, 4), sm(st, 3), AF.Exp)

        def a_finish(n):
            st = n % 2
            RS, ES, DEN, RINV = sm(st, 2), sm(st, 4), sm(st, 5), sm(st, 6)
            tt(DEN, RS, ES, ALU.add)
            P.op("dve", lambda e: e.reciprocal(RINV.ap, DEN.ap), reads=[DEN], writes=[RINV])

        def b_transposes(n):
            st = n % 2
            for h in range(8):
                for hf in range(2):
                    i = h * 2 + hf
                    tr(ptb[i // 8][:, i % 8, :], PEX[st][:, h, hf * 128:(hf + 1) * 128], ident)

        def b_copies(n):
            o = PTS[:, 0:8, :]
            act(o.w(o.ap.rearrange("p a b -> p (a b)")), ptb[0][:, :, :].w(ptb[0][:, :, :].ap.rearrange("p a b -> p (a b)")), AF.Copy)
            o = PTS[:, 8:16, :]
            act(o.w(o.ap.rearrange("p a b -> p (a b)")), ptb[1][:, :, :].w(ptb[1][:, :, :].ap.rearrange("p a b -> p (a b)")), AF.Copy)

        def b_pv(n):
            st = n % 2
            RINV = sm(st, 6)
            po = PS(6)
            for h in range(8):
                for hf in range(2):
                    kvh = HS[h] // 4
                    mm(po[:, h * 64:(h + 1) * 64], PTS[:, h * 2 + hf, :], VV[:, n + hf, kvh * 64:kvh * 64 + 64],
                       hf == 0, hf == 1)
            yo = YTOK[:, :]
            rb = RINV.ap.unsqueeze(2).broadcast_to([128, 8, 64])
            P.op("dve", lambda e, yo=yo, po=po, rb=rb: e.tensor_tensor(
                yo.ap.rearrange("p (h d) -> p h d", h=8), po[:, :].ap.rearrange("p (h d) -> p h d", h=8), rb, ALU.mult),
                reads=[po[:, :], RINV], writes=[yo])

        def b_out(n):
            q0, q1 = n * 128, (n + 1) * 128
            pyt = P.psum(7, [128, 4, 128], BF16)
            for c in range(4):
                tr(pyt[:, c, :], YTOK[:, c * 128:(c + 1) * 128], ident)
            act(QT[:, :, q0:q1], pyt[:, :, :], AF.Copy)

        a_scores(0)
        a_softmax1(0)
        a_exp(0)
        for n in range(16):
            nxt = n + 1 < 16
            if nxt:
                a_scores(n + 1)
            b_transposes(n)
            b_copies(n)
            if nxt:
                a_softmax1(n + 1)
            a_finish(n)
            b_pv(n)
            if nxt:
                a_exp(n + 1)
            b_out(n)

        P.op("dve", lambda e: e.memset(ZERO[:, :].ap, 0.0), writes=[ZERO[:, :]])
        gbd = cb("gbd")

        def gb(ax, c):
            o = ((l * 2 + ax) * 2 + c) * 128
            return CBK[:, CB["gbd"][0] + o:CB["gbd"][0] + o + 128]

        lp = [0]

        def nps6():
            b = lp[0] % 6
            lp[0] += 1
            return PS(b)

        for c in range(2):
            w = nextw(CH_LX[c])
            cpy(LXB[:, 0:3], HSEL[:, 384 + 3 * c:387 + 3 * c])
            for t4 in range(NTT):
                c0, c1 = t4 * 512, (t4 + 1) * 512
                pb = nps6()
                for kc in range(8):
                    mm(pb[:, :], w[:, kc, :], H[:, kc, c0:c1], kc == 0, kc == 7)
                act(LXB[:, 3 + c0:3 + c1], pb[:, :], AF.Copy)
            wg = nextw(CH_LG[c])
            cw = lambda k, c=c: cp("convw", (l * 2 + c) * 4 + k, (l * 2 + c) * 4 + k + 1)

            def s1(t4, c=c, wg=wg, cw=cw):
                c0, c1 = t4 * 512, (t4 + 1) * 512
                lxc = LXC[t4 % 2]
                ts(lxc[:, :], LXB[:, c0:c1], cw(0), ALU.mult, cp("convb", l * 2 + c, l * 2 + c + 1), ALU.add)
                for k in range(1, 4):
                    stt(lxc[:, :], LXB[:, c0 + k:c1 + k], cw(k), lxc[:, :], ALU.mult, ALU.add)
                lb = LXCB[t4 % 2]
                act(lb[:, :], lxc[:, :], AF.Copy)
                pr_ = nps6()
                mm(pr_[:, :], gb(0, c), lb[:, :], True, True)
                pi_ = nps6()
                mm(pi_[:, :], gb(1, c), lb[:, :], True, True)
                pg_ = nps6()
                for kc in range(8):
                    mm(pg_[:, :], wg[:, kc, :], H[:, kc, c0:c1], kc == 0, kc == 7)
                return pr_, pi_, pg_

            def s2(t4, pr_, pi_, pg_, c=c):
                c0, c1 = t4 * 512, (t4 + 1) * 512
                lxc = LXC[t4 % 2]
                act(RR[:, :], pr_[:, :], AF.Sigmoid, bias=cp("ba", l * 2 + c, l * 2 + c + 1), scale=1.0)
                act(II[:, :], pi_[:, :], AF.Sigmoid, bias=cp("bx", l * 2 + c, l * 2 + c + 1), scale=1.0)
                act(AA[:, :], RR[:, :], AF.Exp, scale=LC[:, l, c, 0:1])
                act(A2[:, :], RR[:, :], AF.Exp, scale=LC[:, l, c, 1:2])
                act(MM[:, :], A2[:, :], AF.Sqrt, bias=cp("one"), scale=-1.0)
                if t4 == 0:
                    ts(MM[:, 0:1], MM[:, 0:1], cp("omf"), ALU.mult, cp("f"), ALU.add)
                tt(UU[:, :], II[:, :], lxc[:, :], ALU.mult)
                tt(UU[:, :], UU[:, :], MM[:, :], ALU.mult)
                hl = HL[t4 % 2]
                pp = PP[t4 % 2]
                if t4 == 0:
                    P.op("dve", lambda e, hl=hl: e.tensor_tensor_scan(hl[:, :].ap, AA[:, :].ap, UU[:, :].ap, 0.0, ALU.mult, ALU.add),
                         reads=[AA[:, :], UU[:, :]], writes=[hl[:, :]])
                    P.op("dve", lambda e, pp=pp: e.tensor_tensor_scan(pp[:, :].ap, AA[:, :].ap, ZERO[:, :].ap, 1.0, ALU.mult, ALU.add),
                         reads=[AA[:, :], ZERO[:, :]], writes=[pp[:, :]])
                else:
                    hp = HL[(t4 - 1) % 2][:, 511:512]
                    ppv = PP[(t4 - 1) % 2][:, 511:512]
                    P.op("dve", lambda e, hl=hl, hp=hp: e.tensor_tensor_scan(hl[:, :].ap, AA[:, :].ap, UU[:, :].ap, hp.ap, ALU.mult, ALU.add),
                         reads=[AA[:, :], UU[:, :], hp], writes=[hl[:, :]])
                    P.op("dve", lambda e, pp=pp, ppv=ppv: e.tensor_tensor_scan(pp[:, :].ap, AA[:, :].ap, ZERO[:, :].ap, ppv.ap, ALU.mult, ALU.add),
                         reads=[AA[:, :], ZERO[:, :], ppv], writes=[pp[:, :]])
                act(GG[:, :], pg_[:, :], AF.Gelu_apprx_tanh)
                tt(YL[:, c, c0:c1], GG[:, :], hl[:, :], ALU.mult)
                tt(GP[:, c, c0:c1], GG[:, :], pp[:, :], ALU.mult)

            cur = s1(0)
            for t4 in range(NTT):
                nxt = s1(t4 + 1) if t4 + 1 < NTT else None
                s2(t4, *cur)
                cur = nxt
            if c == 0:
                P.op("dve", lambda e: e.memset(E2ST[:, :].ap, 0.0), writes=[E2ST[:, :]])
            cpy(E2ST[:, c:c + 1], HL[(NTT - 1) % 2][:, 511:512])
            cpy(E2ST[:, 2 + c:3 + c], PP[(NTT - 1) % 2][:, 511:512])
        if fused:
            d2 = DR(f"e2in{l}", [128, 8], F32)
            d2o = DR(f"e2out{l}", [128 * NG, 8], F32)
            dma("sp", P.dv(f"e2in{l}", d2[:, :]), E2ST[:, :], "e2st")
            collective(f"e2in{l}", f"e2out{l}", "cc2")
            dma("sp", E2ALL[:, :, :], P.dv(f"e2out{l}", d2o.ap().rearrange("(r p) f -> p r f", p=128)), "e2ld")
        else:
            d2 = DR("o_e2", [128, 8], F32, "ExternalOutput")
            out_tokens.append(dma("sp", P.dv("o_e2", d2[:, :]), E2ST[:, :], "e2st"))

        for c in range(2):
            wc = nextw(CH_SC[c])
            cpy(ZB[:, 0:2], HSEL[:, 390 + 2 * c:392 + 2 * c])
            for t4 in range(NTT):
                c0, c1 = t4 * 512, (t4 + 1) * 512
                pb = nps()
                for kc in range(8):
                    mm(pb[:, :], wc[:, kc, :], H[:, kc, c0:c1], kc == 0, kc == 7)
                act(ZB[:, 2 + c0:2 + c1], pb[:, :], AF.Copy)
            wx = nextw(CH_SX[c])
            for t4 in range(NTT):
                c0, c1 = t4 * 512, (t4 + 1) * 512
                pb = nps()
                for kc in range(8):
                    mm(pb[:, :], wx[:, kc, :], H[:, kc, c0:c1], kc == 0, kc == 7)
                tt(ZB[:, 2 + c0:2 + c1], pb[:, :], ZB[:, 2 + c0:2 + c1], ALU.mult)
            wb = nextw(CH_SB[c])
            for t4 in range(NTT):
                c0, c1 = t4 * 512, (t4 + 1) * 512
                zc = ZC[t4 % 2]
                sw = lambda k: cp("scw", (l * 2 + c) * 3 + k, (l * 2 + c) * 3 + k + 1)
                ts(zc[:, :], ZB[:, c0:c1], sw(0), ALU.mult)
                for k in range(1, 3):
                    stt(zc[:, :], ZB[:, c0 + k:c1 + k], sw(k), zc[:, :], ALU.mult, ALU.add)
                pb = nps()
                for kc in range(8):
                    mm(pb[:, :], wb[:, kc, :], H[:, kc, c0:c1], kc == 0, kc == 7)
                tt(YS[:, c, c0:c1], pb[:, :], zc[:, :], ALU.mult)

        if not fused:
            dy = DR("o_ystate", [128, 10, T], F32, "ExternalOutput")
            for i, (buf, c) in enumerate([(QT, 0), (QT, 1), (QT, 2), (QT, 3), (YL, 0), (YL, 1), (GP, 0), (GP, 1), (YS, 0), (YS, 1)]):
                out_tokens.append(dma("pool", P.dv("o_ystate", dy[:, i, :]), buf[:, c, :], f"yst{i}"))

    def mix_c(l):
        if not fused:
            dy = DR("i_ystate", [128, 10, T], F32, "ExternalInput")
            for i, (buf, c) in enumerate([(QT, 0), (QT, 1), (QT, 2), (QT, 3), (YL, 0), (YL, 1), (GP, 0), (GP, 1), (YS, 0), (YS, 1)]):
                dma("pool", buf[:, c, :], P.dv("i_ystate", dy[:, i, :], False), f"yld{i}")
            d2o = DR("i_e2all", [128 * NG, 8], F32, "ExternalInput")
            dma("sp", E2ALL[:, :, :], P.dv("i_e2all", d2o.ap().rearrange("(r p) f -> p r f", p=128), False), "e2ld")
        sel = cp("selE2")
        oms = cp("omselE2")
        P.op("dve", lambda e: e.tensor_tensor(E2PM[:, :, :].ap, E2ALL[:, :, 2:4].ap, sel.ap.rearrange("p (a b) -> p a b", b=2), ALU.mult),
             reads=[E2ALL[:, :, 2:4], sel], writes=[E2PM[:, :, :]])
        P.op("dve", lambda e: e.tensor_tensor(E2PM[:, :, :].ap, E2PM[:, :, :].ap, oms.ap.rearrange("p (a b) -> p a b", b=2), ALU.add),
             reads=[E2PM[:, :, :], oms], writes=[E2PM[:, :, :]])
        P.op("dve", lambda e: e.tensor_tensor(E2HM[:, :, :].ap, E2ALL[:, :, 0:2].ap, sel.ap.rearrange("p (a b) -> p a b", b=2), ALU.mult),
             reads=[E2ALL[:, :, 0:2], sel], writes=[E2HM[:, :, :]])
        P.op("dve", lambda e: e.memset(HIN[:, :].ap, 0.0), writes=[HIN[:, :]])
        for r in range(NG):
            tt(HIN[:, :], HIN[:, :], E2PM[:, r, :], ALU.mult)
            tt(HIN[:, :], HIN[:, :], E2HM[:, r, :], ALU.add)
        for c in range(2):
            for t4 in range(NTT):
                c0, c1 = t4 * 512, (t4 + 1) * 512
                stt(YL[:, c, c0:c1], GP[:, c, c0:c1], HIN[:, c:c + 1], YL[:, c, c0:c1], ALU.mult, ALU.add)

        oc = 0
        for dc in range(8):
            wo = WOUT[dc % 2]
            dma("pool", wo[:, :, :], P.dv("wout", d_wout()[l, dc], False), f"wout{dc % 2}")
            for t4 in range(NTT):
                c0, c1 = t4 * 512, (t4 + 1) * 512
                po = PS(4 + oc % 2)
                oc += 1
                for kc in range(8):
                    if kc < 4:
                        rhs = QT[:, kc, c0:c1]
                    elif kc < 6:
                        rhs = YL[:, kc - 4, c0:c1]
                    else:
                        rhs = YS[:, kc - 6, c0:c1]
                    mm(po[:, :], wo[:, kc, :], rhs, kc == 0, kc == 7)
                stt(X[:, dc, c0:c1], po[:, :], GT[:, l, 1, dc:dc + 1], X[:, dc, c0:c1], ALU.mult, ALU.add)

    pro_common()
    only_pro = (list(stages) == [("pro",)])
    if fused or only_pro:
        pro_modparts()
    if not only_pro:
        pro_modfinish()
    pre = False
    fin_done = False
    for si, st in enumerate(stages):
        if st[0] == "ffn":
            nn = None
            if fused and si + 1 < len(stages):
                nx = stages[si + 1]
                if nx[0] == "ffn":
                    nn = (nx[1], nx[2], False)
                elif nx[0] == "mix":
                    nn = (nx[1], 1, False)
                elif nx[0] == "fin":
                    nn = (0, 0, True)
                    fin_done = True
            ffn(st[1], st[2], pre_normed=pre, next_norm=nn)
            pre = nn is not None
        elif st[0] == "mix":
            mix_a(st[1], pre_normed=pre)
            pre = False
            mix_b(st[1])
            mix_c(st[1])
        elif st[0] == "ma":
            mix_a(st[1])
        elif st[0] == "mb":
            mix_b(st[1])
        elif st[0] == "mc":
            mix_c(st[1])
    d_out = None
    if ("fin",) in stages:
        d_out = DR("outT", [D, T], F32, "ExternalOutput")
        if not fin_done:
            for t4 in range(NTT):
                norm_tile(0, 0, t4, final=True)
    elif not only_pro and stages[-1][0] != "mb":
        d_out = DR("outT", [D, T], F32, "ExternalOutput")
        for dc in range(8):
            dma("sp", P.dv("outT", d_out[dc * 128:(dc + 1) * 128, :]), X[:, dc, :], f"xst{dc}")
    finals = list(out_tokens)
    for sname, v in P.slot_val.items():
        if sname.startswith("ost") or sname.startswith("xst"):
            finals.append(("d", sname, v))
    P.finalize(finals)
    return P


def _alibi_table():
    slopes = (2.0 ** (-8.0 * np.arange(1, 9) / 8)).astype(np.float32)
    qi = np.arange(128)[:, None]
    kj = np.arange(256)[None, :]
    dist = qi + 128 - kj
    valid = (dist >= 0) & (dist < 128)
    tab = np.where(valid[None], -slopes[:, None, None] * dist[None].astype(np.float32), np.float32(-1e30))
    tab = tab[HS]
    return np.ascontiguousarray(tab.transpose(1, 0, 2)).astype(np.float32)


def prepare_inputs(inp):
    f = np.float32
    x = np.asarray(inp["x"], f)
    c = np.asarray(inp["c"], f)
    w_mod = np.asarray(inp["w_mod"], f)
    b_mod = np.asarray(inp["b_mod"], f)
    shared = {}
    for s, name in enumerate(["w_ffn1_gu", "w_ffn2_gu"]):
        w = np.asarray(inp[name], f)
        g = w[:, :, :DFF].reshape(DEPTH, 8, 128, NJ, 128)
        u = w[:, :, DFF:].reshape(DEPTH, 8, 128, NJ, 128)
        gu = np.stack([g, u], axis=4)
        shared[f"wgu{s}"] = np.ascontiguousarray(gu.transpose(0, 3, 2, 1, 4, 5)).reshape(DEPTH, NJ, 128, 8, 256)
    shared["wdn0"] = np.ascontiguousarray(np.asarray(inp["w_ffn1_down"], f))
    shared["wdn1"] = np.ascontiguousarray(np.asarray(inp["w_ffn2_down"], f))
    w_in = np.asarray(inp["w_in"], f)
    cols = []
    for cq in range(4):
        cols.append(np.arange(cq * 128, (cq + 1) * 128))
    cols.append(np.concatenate([np.arange(512, 576), np.arange(512, 576)]))
    cols.append(np.concatenate([np.arange(576, 640), np.arange(576, 640)]))
    cols.append(np.arange(640, 768))
    for base in (768, 1024, 1280, 1536, 1792):
        cols.append(np.arange(base, base + 128))
        cols.append(np.arange(base + 128, base + 256))
    colidx = np.stack(cols)
    wi = w_in[:, :, colidx]
    wi = wi.reshape(DEPTH, 8, 128, NCH_IN, 128).transpose(0, 3, 2, 1, 4)
    shared["win"] = np.ascontiguousarray(wi)
    wo = np.asarray(inp["w_out"], f).copy()
    attn_rows = np.concatenate([np.arange(HS[sl] * 64, HS[sl] * 64 + 64) for sl in range(8)])
    wo[:, :512, :] = wo[:, attn_rows, :]
    wo = wo.reshape(DEPTH, 8, 128, 8, 128).transpose(0, 3, 2, 1, 4)
    shared["wout"] = np.ascontiguousarray(wo)

    cbk = np.zeros((128, NCB), f)
    cbk[:, CB["ident"][0]:CB["ident"][1]] = np.eye(128, dtype=f)
    cbk[:, CB["ones"][0]:CB["ones"][1]] = 1.0
    ga = np.asarray(inp["lru_gate_a_w"], f)
    gx = np.asarray(inp["lru_gate_x_w"], f)
    gbd = np.zeros((128, DEPTH, 2, 2, 128), f)
    for l in range(DEPTH):
        for ax, gw in enumerate((ga, gx)):
            for cc in range(2):
                for hh in range(2):
                    gbd[hh * 64:(hh + 1) * 64, l, ax, cc, hh * 64:(hh + 1) * 64] = gw[l, cc * 2 + hh]
    cbk[:, CB["gbd"][0]:CB["gbd"][1]] = gbd.reshape(128, -1)
    cbk[:, CB["bias"][0]:CB["bias"][1]] = _alibi_table().reshape(128, -1)
    shared["cpackb"] = cbk

    def colT(a):
        a = np.asarray(a, f)
        lead = a.shape[:-1]
        return np.moveaxis(a.reshape(*lead, 2, 128), -1, 0)

    in_maps = []
    for core in range(NCORES):
        b, ch = core // 4, core % 4
        m = dict(shared)
        m["xT"] = np.ascontiguousarray(x[b, ch * T:(ch + 1) * T, :].T)
        cpk = np.zeros((128, NCP), f)

        def put(name, arr):
            a, bb = CP[name]
            cpk[:, a:bb] = np.asarray(arr, f).reshape(128, bb - a)

        put("cT", c[b].reshape(8, 128).T)
        put("gn", np.asarray(inp["g_norm"], f).reshape(DEPTH, 3, 8, 128).transpose(3, 0, 1, 2))
        put("gfin", np.asarray(inp["g_final"], f).reshape(8, 128).T)
        put("convw", colT(np.asarray(inp["lru_conv_w"], f)).transpose(0, 1, 3, 2))
        put("convb", colT(inp["lru_conv_b"]))
        put("ba", colT(inp["lru_gate_a_b"]))
        put("bx", colT(inp["lru_gate_x_b"]))
        put("lam", colT(inp["lru_lambda"]))
        put("scw", colT(np.asarray(inp["sc_conv_w"], f)).transpose(0, 1, 3, 2))
        put("sinks", np.broadcast_to(np.asarray(inp["attn_sinks"], f)[:, HS].reshape(1, 16), (128, 16)))
        se1 = np.zeros(NG, f)
        if ch > 0:
            se1[ch - 1] = 1.0
        put("selE1", np.broadcast_to(se1, (128, NG)))
        se2 = np.zeros(NG, f)
        se2[:ch] = 1.0
        put("selE2", np.broadcast_to(np.repeat(se2, 2), (128, 2 * NG)))
        put("omselE2", np.broadcast_to(np.repeat(1.0 - se2, 2), (128, 2 * NG)))
        first = 1.0 if ch == 0 else 0.0
        put("f", np.full((128, 1), first, f))
        put("omf", np.full((128, 1), 1.0 - first, f))
        put("negbig", np.full((128, 1), -1e30 if ch == 0 else 0.0, f))
        put("eps", np.full((128, 1), 1e-6, f))
        put("one", np.ones((128, 1), f))
        put("zero", np.zeros((128, 1), f))
        put("identf", np.eye(128, dtype=f))
        m["cpack"] = cpk
        wm = w_mod[:, :, ch * 2304:(ch + 1) * 2304].reshape(DEPTH, 8, 128, 2304).transpose(0, 2, 1, 3)
        m["wmod"] = np.ascontiguousarray(wm)
        m["bmod"] = np.ascontiguousarray(b_mod[None, :, ch * 2304:(ch + 1) * 2304])
        in_maps.append(m)
    return in_maps


FULL_STAGES = [("ffn", 0, 0), ("mix", 0), ("ffn", 0, 2), ("ffn", 1, 0), ("mix", 1), ("ffn", 1, 2), ("fin",)]
LAUNCHES = [
    [("pro",)],
    [("ffn", 0, 0), ("ma", 0)],
    [("mb", 0)],
    [("mc", 0), ("ffn", 0, 2), ("ffn", 1, 0), ("ma", 1)],
    [("mb", 1)],
    [("mc", 1), ("ffn", 1, 2), ("fin",)],
]
USE_FUSED = True
_PROG_CACHE = {}


def _get_prog(stages, fused):
    key = (tuple(stages), fused)
    if key not in _PROG_CACHE:
        _PROG_CACHE[key] = build_program(list(stages), fused=fused)
    return _PROG_CACHE[key]


def _launch(stages, fused, base, extra, trace=False):
    P = _get_prog(stages, fused)
    in_maps = []
    for c in range(NCORES):
        m = {}
        for k in P.ext_in:
            m[k] = extra[c][k] if k in extra[c] else base[c][k]
        in_maps.append(m)
    res = run_bass_kernel_spmd(P.nc, in_maps, core_ids=list(range(NCORES)), trace=trace)
    return res


def _assemble(results):
    out = np.empty((2, 4 * T, D), np.float32)
    for core in range(NCORES):
        b, ch = core // 4, core % 4
        out[b, ch * T:(ch + 1) * T, :] = np.asarray(results[core]["outT"]).reshape(D, T).T
    return out


def run_stages(inp, stages, trace=False, fused=True):
    base = prepare_inputs(inp)
    res = _launch(stages, fused, base, [{} for _ in range(NCORES)], trace=trace)
    return _assemble(res.results), res


def run_unfused(inp, upto=None):
    base = prepare_inputs(inp)
    cores = range(NCORES)
    extra = [{} for _ in cores]
    r = _launch(LAUNCHES[0], False, base, extra).results

    def gather(key, shape):
        for g in range(NCORES // NG):
            grp = range(g * NG, (g + 1) * NG)
            yield grp, np.ascontiguousarray(np.concatenate([np.asarray(key(c)).reshape(shape) for c in grp], axis=0))

    for grp, arr in gather(lambda c: r[c]["o_modpart"], (36, 128)):
        for c in grp:
            extra[c]["i_modall"] = arr
    res = None
    for li, st in enumerate(LAUNCHES[1:]):
        if upto is not None and li >= upto:
            break
        res = _launch(st, False, base, extra).results
        if "outT" in res[0]:
            for c in cores:
                extra[c]["xT"] = np.ascontiguousarray(res[c]["outT"]).reshape(D, T)
        if "o_e1" in res[0]:
            for grp, arr in gather(lambda c: res[c]["o_e1"], (128, 400)):
                for c in grp:
                    extra[c]["i_e1all"] = arr
        if "o_e2" in res[0]:
            for grp, arr in gather(lambda c: res[c]["o_e2"], (128, 8)):
                for c in grp:
                    extra[c]["i_e2all"] = arr
            for c in cores:
                extra[c]["i_ystate"] = np.ascontiguousarray(res[c]["o_ystate"]).reshape(128, 10, T)
    return res, extra


def kernel(**inputs):
    if USE_FUSED:
        out, _ = run_stages(inputs, FULL_STAGES, fused=True)
        return out
    res, _ = run_unfused(inputs)
    return _assemble(res)
```

```python
import numpy as np
import concourse.bass as bass
import concourse.mybir as mybir
from concourse.bass_utils import run_bass_kernel_spmd

F32 = mybir.dt.float32
BF16 = mybir.dt.bfloat16
AF = mybir.ActivationFunctionType
ALU = mybir.AluOpType
AX = mybir.AxisListType

NCORES = 8
NG = 4
T = 2048
NTT = 4
D = 1024
DFF = 2816
NJ = 22
GROUPS = [(0, 11), (11, 22)]
DEPTH = 2
ESZ = {F32: 4, BF16: 2}

CH_Q = [0, 1, 2, 3]
CH_K = [4, 5]
CH_V = 6
CH_LX = [7, 8]
CH_LG = [9, 10]
CH_SB = [11, 12]
CH_SC = [13, 14]
CH_SX = [15, 16]
NCH_IN = 17
HS = [0, 2, 4, 6, 1, 3, 5, 7]

CP = {}
_o = 0
for _n, _w in [("cT", 8), ("gn", 48), ("gfin", 8), ("convw", 16), ("convb", 4), ("ba", 4), ("bx", 4),
               ("lam", 4), ("scw", 12), ("sinks", 16), ("selE1", 4), ("selE2", 8), ("omselE2", 8),
               ("f", 1), ("omf", 1), ("negbig", 1), ("eps", 1), ("one", 1), ("zero", 1),
               ("identf", 128)]:
    CP[_n] = (_o, _o + _w)
    _o += _w
NCP = _o
CB = {}
_o = 0
for _n, _w in [("ident", 128), ("ones", 128), ("gbd", 1024), ("bias", 2048)]:
    CB[_n] = (_o, _o + _w)
    _o += _w
NCB = _o


class V:
    __slots__ = ("ap", "reg")

    def __init__(self, ap, reg):
        self.ap = ap
        self.reg = reg

    def w(self, ap):
        return V(ap, self.reg)


class Buf:
    def __init__(self, handle, shape, dtype, space, base):
        self.h = handle
        self.shape = list(shape)
        self.dtype = dtype
        self.space = space
        self.base = base
        self.esz = ESZ[dtype]
        st = [1] * len(shape)
        for i in range(len(shape) - 2, 0, -1):
            st[i] = st[i + 1] * shape[i + 1]
        self.strides = st

    def __getitem__(self, idx):
        if not isinstance(idx, tuple):
            idx = (idx,)
        idx = list(idx) + [slice(None)] * (len(self.shape) - len(idx))
        rng = []
        for d, i in enumerate(idx):
            if isinstance(i, int):
                rng.append((i, i + 1))
            else:
                lo = 0 if i.start is None else i.start
                hi = self.shape[d] if i.stop is None else i.stop
                rng.append((lo, hi))
        p0, p1 = rng[0]
        fr = rng[1:]
        sh = self.shape[1:]
        st = self.strides[1:]
        k = len(fr) - 1
        while k > 0 and fr[k] == (0, sh[k]):
            k -= 1
        if len(fr) == 0:
            ivs = [(self.base, self.base + self.esz)]
        else:
            blk0 = fr[k][0] * st[k]
            blk1 = fr[k][1] * st[k]
            outer = [0]
            for d in range(k):
                outer = [o + i * st[d] for o in outer for i in range(fr[d][0], fr[d][1])]
            if len(outer) > 64:
                lo = min(outer) + blk0
                hi = max(outer) + blk1
                ivs = [(self.base + lo * self.esz, self.base + hi * self.esz)]
            else:
                ivs = [(self.base + (o + blk0) * self.esz, self.base + (o + blk1) * self.esz) for o in outer]
        return V(self.h[tuple(idx)], (self.space, p0, p1, ivs))


class Prog:
    ENGS = ("pe", "act", "dve", "pool", "sp")

    def __init__(self):
        self.nc = bass.Bass("TRN2", target_bir_lowering=False)
        self.ops = {e: [] for e in self.ENGS}
        self.recs = {}
        self.slot_val = {}
        self.slot_inc = {}
        self.sb_lo = ((self.nc.sbuf_base + 63) // 64) * 64
        self.sb_hi = self.nc.sbuf_top
        self.sb_ptr = self.sb_lo
        self.nalloc = 0
        self.ps = []
        for b in range(8):
            h = self.nc.alloc_psum_tensor(f"psb{b}", [128, 512], F32)
            self.ps.append(h)
        self.drams = {}

    def sb(self, name, shape, dtype, at=None):
        nbytes = int(np.prod(shape[1:])) * ESZ[dtype]
        nbytes = ((nbytes + 63) // 64) * 64
        if at is None:
            at = self.sb_ptr
            self.sb_ptr += nbytes
        assert at % 32 == 0
        assert at + nbytes <= self.sb_hi, (name, at, nbytes, self.sb_hi)
        self.nalloc += 1
        h = self.nc.alloc_sbuf_tensor_at(f"{name}_{self.nalloc}", list(shape), dtype, offset=at)
        b = Buf(h, shape, dtype, "sb", at)
        b.nbytes = nbytes
        return b

    def psum(self, bank, shape, dtype=F32):
        h = self.ps[bank]
        if dtype == BF16:
            h = h.bitcast(BF16)
            full = [128, 1024]
        else:
            full = [128, 512]
        if list(shape) != full:
            names = " ".join(f"d{i}" for i in range(len(shape) - 1))
            kw = {f"d{i}": shape[i + 1] for i in range(len(shape) - 1)}
            need = int(np.prod(shape[1:]))
            if need != full[1]:
                h = h[:, 0:need]
            h = h.rearrange(f"p ({names}) -> p {names}", **kw)
        return Buf(h, shape, dtype, "ps", bank * 2048)

    def dram(self, name, shape, dtype, kind=None):
        if kind is None:
            h = self.nc.dram_tensor(name, list(shape), dtype)
        else:
            h = self.nc.dram_tensor(name, list(shape), dtype, kind=kind)
        self.drams[name] = h
        return h

    def dv(self, name, ap, track=True):
        if not track:
            return V(ap, None)
        return V(ap, (("dram", name), 0, 1, [(0, 1)]))

    def _buckets(self, reg):
        space, p0, p1, ivs = reg
        out = []
        for (b0, b1) in ivs:
            for k in range(b0 // 1024, (b1 - 1) // 1024 + 1):
                out.append((space, k))
        return out

    @staticmethod
    def _overlap(r1, r2):
        if r1[1] >= r2[2] or r2[1] >= r1[2]:
            return False
        for (a0, a1) in r1[3]:
            for (b0, b1) in r2[3]:
                if a0 < b1 and b0 < a1:
                    return True
        return False

    @staticmethod
    def _covers(big, small):
        if big[1] > small[1] or big[2] < small[2]:
            return False
        for (s0, s1) in small[3]:
            ok = False
            for (b0, b1) in big[3]:
                if b0 <= s0 and s1 <= b1:
                    ok = True
                    break
            if not ok:
                return False
        return True

    def _query(self, reg):
        seen = set()
        res = []
        for bk in self._buckets(reg):
            for rec in self.recs.get(bk, ()):
                if id(rec) in seen or rec["dead"]:
                    continue
                seen.add(id(rec))
                if self._overlap(rec["reg"], reg):
                    res.append(rec)
        return res

    def _insert(self, rec):
        for bk in self._buckets(rec["reg"]):
            self.recs.setdefault(bk, []).append(rec)

    def _gc(self, reg):
        for bk in self._buckets(reg):
            lst = self.recs.get(bk)
            if lst and len(lst) > 24:
                self.recs[bk] = [r for r in lst if not r["dead"]]

    def op(self, eng, fn, reads=(), writes=(), dma=None, inc=16):
        idx = len(self.ops[eng])
        if dma is not None:
            self.slot_inc.setdefault(dma, inc)
            assert self.slot_inc[dma] == inc
            val = self.slot_val.get(dma, 0) + inc
            self.slot_val[dma] = val
            tok = ("d", dma, val)
            tok_eng = "dma:" + dma
        else:
            tok = ("e", eng, idx)
            tok_eng = eng
        deps = set()
        for v in reads:
            if v is None or v.reg is None:
                continue
            for rec in self._query(v.reg):
                if rec["kind"] == "W":
                    deps.add(rec["tok"])
        for v in writes:
            if v is None or v.reg is None:
                continue
            for rec in self._query(v.reg):
                if rec["eng"] == tok_eng and dma is None and eng == "pe":
                    if self._covers(v.reg, rec["reg"]):
                        rec["dead"] = True
                    continue
                deps.add(rec["tok"])
                if self._covers(v.reg, rec["reg"]):
                    rec["dead"] = True
        for v in reads:
            if v is None or v.reg is None:
                continue
            found = False
            for rec in self._query(v.reg):
                if rec["kind"] == "R" and rec["eng"] == tok_eng and rec["reg"] == v.reg and dma is None:
                    rec["tok"] = tok
                    found = True
                    break
            if not found:
                self._insert({"reg": v.reg, "kind": "R", "tok": tok, "eng": tok_eng, "dead": False})
        for v in writes:
            if v is None or v.reg is None:
                continue
            self._insert({"reg": v.reg, "kind": "W", "tok": tok, "eng": tok_eng, "dead": False})
            self._gc(v.reg)
        deps.discard(tok)
        self.ops[eng].append({"fn": fn, "deps": deps, "dma": dma, "inc": inc, "sig": False})
        return tok

    def finalize(self, final_waits):
        nc = self.nc
        for e in self.ENGS:
            for o in self.ops[e]:
                for t in o["deps"]:
                    if t[0] == "e":
                        self.ops[t[1]][t[2]]["sig"] = True
        counts = {}
        for e in self.ENGS:
            c = 0
            lst = []
            for o in self.ops[e]:
                if o["sig"] and o["dma"] is None:
                    c += 1
                lst.append(c)
            counts[e] = lst
        esem = {e: nc.alloc_semaphore(f"sem_{e}") for e in self.ENGS}
        dsem = {s: nc.alloc_semaphore(f"dsem_{s}") for s in self.slot_val}
        nwaits = 0
        engobj = {"pe": "tensor", "act": "scalar", "dve": "vector", "pool": "gpsimd", "sp": "sync"}

        def resolve(t):
            if t[0] == "e":
                return esem[t[1]], counts[t[1]][t[2]], ("e", t[1])
            return dsem[t[1]], t[2], ("d", t[1])

        def emit(e, engine):
            nonlocal nwaits
            waited = {}
            for i, o in enumerate(self.ops[e]):
                need = {}
                for t in o["deps"]:
                    sem, val, key = resolve(t)
                    if key == ("e", e) and t[2] >= i:
                        raise RuntimeError("forward dep")
                    if waited.get(key, 0) >= val:
                        continue
                    if need.get(key, (None, 0))[1] < val:
                        need[key] = (sem, val)
                for key, (sem, val) in need.items():
                    engine.wait_ge(sem, val)
                    waited[key] = val
                    nwaits += 1
                ins = o["fn"](engine)
                if o["dma"] is not None:
                    ins.then_inc(dsem[o["dma"]], o["inc"])
                elif o["sig"]:
                    ins.then_inc(esem[e], 1)
            if e == "sp":
                for t in final_waits:
                    sem, val, key = resolve(t)
                    engine.wait_ge(sem, val)

        with nc.Block() as block:
            @block.tensor
            def _(eng):
                emit("pe", eng)

            @block.scalar
            def _(eng):
                emit("act", eng)

            @block.vector
            def _(eng):
                emit("dve", eng)

            @block.gpsimd
            def _(eng):
                emit("pool", eng)

            @block.sync
            def _(eng):
                emit("sp", eng)
        self.stats = {e: len(self.ops[e]) for e in self.ENGS}
        self.stats["waits"] = nwaits
        return nc


def build_program(stages, fused=True):
    P = Prog()
    nc = P.nc
    P.ext_in = []
    P.ext_out = []
    _dr = {}

    def DR(name, shape=None, dtype=F32, kind=None):
        if name not in _dr:
            _dr[name] = P.dram(name, shape, dtype, kind)
            if kind == "ExternalInput":
                P.ext_in.append(name)
            elif kind == "ExternalOutput":
                P.ext_out.append(name)
        return _dr[name]

    d_xT = DR("xT", [D, T], F32, "ExternalInput")
    d_cp = DR("cpack", [128, NCP], F32, "ExternalInput")
    d_cb = DR("cpackb", [128, NCB], F32, "ExternalInput")
    d_wgu = lambda fi: DR(f"wgu{fi}", [DEPTH, NJ, 128, 8, 256], F32, "ExternalInput")
    d_wdn = lambda fi: DR(f"wdn{fi}", [DEPTH, DFF, D], F32, "ExternalInput")
    d_win = lambda: DR("win", [DEPTH, NCH_IN, 128, 8, 128], F32, "ExternalInput")
    d_wout = lambda: DR("wout", [DEPTH, 8, 128, 8, 128], F32, "ExternalInput")
    out_tokens = []
    RG = [[0, 1, 2, 3], [4, 5, 6, 7]]

    X = P.sb("X", [128, 8, T], F32)
    H = P.sb("H", [128, 8, T], BF16)
    CPK = P.sb("CPK", [128, NCP], F32)
    CBK = P.sb("CBK", [128, NCB], BF16)
    MT = P.sb("MT", [128, DEPTH, 72], F32)
    AM = P.sb("AM", [128, DEPTH, 3, 8], F32)
    GT = P.sb("GT", [128, DEPTH, 3, 8], F32)
    LC = P.sb("LC", [128, DEPTH, 2, 2], F32)
    CACT = P.sb("CACT", [128, 8, 1], BF16)
    HSEL = P.sb("HSEL", [128, 400], F32)
    E2ST = P.sb("E2ST", [128, 8], F32)
    HIN = P.sb("HIN", [128, 2], F32)
    SMALL = P.sb("SMALL", [128, 128], F32)
    arena0 = P.sb_ptr
    arena_sz = P.sb_hi - arena0

    def cp(name, lo=None, hi=None):
        a, b = CP[name]
        if lo is None:
            return CPK[:, a:b]
        return CPK[:, a + lo:a + hi]

    def cb(name, lo=None, hi=None):
        a, b = CB[name]
        if lo is None:
            return CBK[:, a:b]
        return CBK[:, a + lo:a + hi]

    class Arena:
        def __init__(self, base):
            self.p = base

        def sb(self, name, shape, dtype):
            b = P.sb(name, shape, dtype, at=self.p)
            self.p += b.nbytes
            assert self.p <= P.sb_hi, (name, self.p - P.sb_hi)
            return b

    fa = Arena(arena0)
    ACTB = fa.sb("ACTB", [128, 11, T], BF16)
    WGU = [fa.sb(f"WGU{i}", [128, 8, 256], BF16) for i in range(3)]
    WD = fa.sb("WD", [128, 11, D], BF16)
    SG = [fa.sb(f"SG{i}", [128, 512], F32) for i in range(2)]
    XSQ = [fa.sb(f"XSQ{i}", [128, 512], BF16) for i in range(8)]
    TMPN = [fa.sb(f"TMPN{i}", [128, 512], F32) for i in range(2)]
    LNT = fa.sb("LNT", [128, 512], F32)
    ffn_end = fa.p
    pa = Arena(arena0)
    WM = pa.sb("WM", [128, 8, 2304], BF16)
    BM = pa.sb("BM", [1, DEPTH, 2304], F32)
    MP = pa.sb("MP", [1, DEPTH, 2304], F32)
    MR = [pa.sb(f"MR{i}", [72, 128], F32) for i in range(2)]
    ma = Arena(arena0)
    QT = ma.sb("QT", [128, 4, T], BF16)
    YL = ma.sb("YL", [128, 2, T], BF16)
    GP = ma.sb("GP", [128, 2, T], BF16)
    YS = ma.sb("YS", [128, 2, T], BF16)
    WIN = [ma.sb(f"WIN{i}", [128, 8, 128], BF16) for i in range(3)]
    WOUT = [ma.sb(f"WOUT{i}", [128, 8, 128], BF16) for i in range(2)]
    HST = ma.sb("HST", [128, 400], F32)
    E2ALL = ma.sb("E2ALL", [128, NG, 8], F32)
    E2PM = ma.sb("E2PM", [128, NG, 2], F32)
    E2HM = ma.sb("E2HM", [128, NG, 2], F32)
    phase0 = ma.p
    aa = Arena(phase0)
    KT = aa.sb("KT", [128, 2, 128 + T], BF16)
    VV = aa.sb("VV", [128, 17, 128], BF16)
    SSB = aa.sb("SSB", [128, 8, 256], F32)
    PEXP = aa.sb("PEXP", [128, 8, 256], BF16)
    PEXP1 = aa.sb("PEXP1", [128, 8, 256], BF16)
    PTS = aa.sb("PTS", [128, 16, 128], BF16)
    YTOK = aa.sb("YTOK", [128, 512], BF16)
    HALL = aa.sb("HALL", [128, NG, 400], F32)
    SSB1 = P.sb("SSB1", [128, 8, 256], F32, at=HALL.base)
    aa.p = max(aa.p, HALL.base + SSB1.nbytes)
    assert aa.p <= P.sb_hi
    la = Arena(phase0)
    LXB = la.sb("LXB", [128, 3 + T], F32)
    LXC = [la.sb(f"LXC{i}", [128, 512], F32) for i in range(2)]
    LXCB = [la.sb(f"LXCB{i}", [128, 512], BF16) for i in range(2)]
    RR = la.sb("RR", [128, 512], F32)
    II = la.sb("II", [128, 512], F32)
    AA = la.sb("AA", [128, 512], F32)
    A2 = la.sb("A2", [128, 512], F32)
    MM = la.sb("MM", [128, 512], F32)
    UU = la.sb("UU", [128, 512], F32)
    HL = [la.sb(f"HL{i}", [128, 512], F32) for i in range(2)]
    PP = [la.sb(f"PP{i}", [128, 512], F32) for i in range(2)]
    GG = la.sb("GG", [128, 512], F32)
    ZERO = la.sb("ZERO", [128, 512], BF16)
    TMPC = la.sb("TMPC", [128, 512], F32)
    sa = Arena(phase0)
    ZB = sa.sb("ZB", [128, 2 + T], F32)
    ZC = [sa.sb(f"ZC{i}", [128, 512], F32) for i in range(2)]
    TMPZ = sa.sb("TMPZ", [128, 512], F32)

    ident = cb("ident")
    onesb = cb("ones")

    PS = lambda b: P.psum(b, [128, 512], F32)

    def mm(out, lhsT, rhs, start, stop):
        P.op("pe", lambda e: e.matmul(out.ap, lhsT.ap, rhs.ap, start=start, stop=stop),
             reads=[lhsT, rhs], writes=[out])

    def tr(out, in_, idv):
        P.op("pe", lambda e: e.transpose(out.ap, in_.ap, idv.ap), reads=[in_, idv], writes=[out])

    def act(out, in_, func, bias=None, scale=None, accum=None):
        rd = [in_]
        kw = {}
        if bias is not None:
            if isinstance(bias, V):
                rd.append(bias)
                kw["bias"] = bias.ap
            else:
                kw["bias"] = bias
        if scale is not None:
            if isinstance(scale, V):
                rd.append(scale)
                kw["scale"] = scale.ap
            else:
                kw["scale"] = scale
        wr = [out]
        if accum is not None:
            kw["accum_out"] = accum.ap
            wr.append(accum)
        P.op("act", lambda e: e.activation(out.ap, in_.ap, func, **kw), reads=rd, writes=wr)

    def tt(out, in0, in1, op, eng="dve"):
        P.op(eng, lambda e: e.tensor_tensor(out.ap, in0.ap, in1.ap, op), reads=[in0, in1], writes=[out])

    def ts(out, in0, s1, op0, s2=None, op1=None, eng="dve"):
        rd = [in0]
        a1 = s1.ap if isinstance(s1, V) else s1
        a2 = s2.ap if isinstance(s2, V) else s2
        if isinstance(s1, V):
            rd.append(s1)
        if isinstance(s2, V):
            rd.append(s2)
        if op1 is None:
            P.op(eng, lambda e: e.tensor_scalar(out.ap, in0.ap, a1, None, op0), reads=rd, writes=[out])
        else:
            P.op(eng, lambda e: e.tensor_scalar(out.ap, in0.ap, a1, a2, op0, op1), reads=rd, writes=[out])

    def stt(out, in0, sc, in1, op0, op1):
        rd = [in0, in1]
        a = sc.ap if isinstance(sc, V) else sc
        if isinstance(sc, V):
            rd.append(sc)
        P.op("dve", lambda e: e.scalar_tensor_tensor(out.ap, in0.ap, a, in1.ap, op0, op1), reads=rd, writes=[out])

    def cpy(out, in_, eng="dve"):
        P.op(eng, lambda e: e.tensor_copy(out.ap, in_.ap), reads=[in_], writes=[out])

    def dma(eng, out, in_, slot):
        return P.op(eng, lambda e: e.dma_start(out=out.ap, in_=in_.ap), reads=[in_], writes=[out], dma=slot)

    def collective(in_name, out_name, slot):
        hin, hout = P.drams[in_name], P.drams[out_name]
        vin = P.dv(in_name, None)
        vout = P.dv(out_name, None)
        return P.op("pool", lambda e: e.collective_compute("AllGather", ALU.bypass, replica_groups=RG,
                                                          ins=[hin.ap().opt()], outs=[hout.ap().opt()]),
                    reads=[vin], writes=[vout], dma=slot, inc=1)

    def pro_common():
        for dc in range(8):
            dma("sp", X[:, dc, :], P.dv("xT", d_xT[dc * 128:(dc + 1) * 128, :], False), f"xld{dc}")
        dma("sp", CPK[:, :], P.dv("cpack", d_cp[:, :], False), "cpk")
        for i, (a, b) in enumerate([(0, 1280), (1280, 2304), (2304, NCB)]):
            dma("pool", CBK[:, a:b], P.dv("cpackb", d_cb[:, a:b], False), f"cbk{i}")

    def pro_modparts():
        d_wmod = DR("wmod", [DEPTH, 128, 8, 2304], F32, "ExternalInput")
        d_bmod = DR("bmod", [1, DEPTH, 2304], F32, "ExternalInput")
        dma("sp", BM[:, :, :], P.dv("bmod", d_bmod[:, :, :], False), "bmld")
        act(CACT[:, :, :].w(CACT[:, :, :].ap.rearrange("p a b -> p (a b)")), cp("cT"), AF.Silu)
        for l in range(DEPTH):
            for kc in range(8):
                for hf in range(2):
                    dma("pool", WM[:, kc, hf * 1152:(hf + 1) * 1152],
                        P.dv("wmod", d_wmod[l, :, kc, hf * 1152:(hf + 1) * 1152], False), f"wm{kc}_{hf}")
            for n6 in range(6):
                pb = P.psum(n6 % 4, [128, 512], F32)
                for kc in range(8):
                    mm(pb[0:1, 0:384], CACT[:, kc, :], WM[:, kc, n6 * 384:(n6 + 1) * 384], kc == 0, kc == 7)
                tt(MP[0:1, l, n6 * 384:(n6 + 1) * 384], pb[0:1, 0:384], BM[0:1, l, n6 * 384:(n6 + 1) * 384], ALU.add)
        if fused:
            d_modpart = DR("modpart", [36, 128], F32)
            pname = "modpart"
        else:
            d_modpart = DR("o_modpart", [36, 128], F32, "ExternalOutput")
            pname = "o_modpart"
        dst_ap = d_modpart.ap().rearrange("(q l) p -> l q p", l=DEPTH)
        for l in range(DEPTH):
            src = MP[0:1, l, :]
            t = dma("sp", P.dv(pname, dst_ap[l:l + 1]), src.w(src.ap.rearrange("b (q p) -> b q p", p=128)), f"mpst{l}")
            if not fused:
                out_tokens.append(t)
        if fused:
            DR("modall", [36 * NG, 128], F32)
            collective("modpart", "modall", "ccm")

    def pro_modfinish():
        if fused:
            d_modall = DR("modall", [36 * NG, 128], F32)
            mname, trk = "modall", True
        else:
            d_modall = DR("i_modall", [36 * NG, 128], F32, "ExternalInput")
            mname, trk = "i_modall", False
        gall = d_modall.ap().rearrange("(j l) p -> l j p", l=DEPTH)
        for l in range(DEPTH):
            dma("sp", MR[l][:, :], P.dv(mname, gall[l], trk), f"mrl{l}")
        identf = cp("identf")
        for l in range(DEPTH):
            pb = P.psum(4 + l, [128, 512], F32)
            tr(pb[:, 0:72], MR[l][:, :], CPK[0:72, CP["identf"][0]:CP["identf"][0] + 72])
            cpy(MT[:, l, :], pb[:, 0:72])
            for s in range(3):
                gn = cp("gn", (l * 3 + s) * 8, (l * 3 + s) * 8 + 8)
                stt(AM[:, l, s, :], MT[:, l, (3 * s + 1) * 8:(3 * s + 1) * 8 + 8], 1.0, gn, ALU.add, ALU.mult)
                ts(GT[:, l, s, :], MT[:, l, (3 * s + 2) * 8:(3 * s + 2) * 8 + 8], 1.0 if s == 1 else 0.5, ALU.mult)
        lam = cp("lam")
        ee = SMALL[:, 0:4]
        pq = SMALL[:, 4:8]
        t2 = SMALL[:, 8:12]
        act(ee, lam, AF.Exp, scale=-1.0)
        ts(pq, ee, -0.2, ALU.mult, 0.25, ALU.add)
        for cst in (1.0 / 3.0, 0.5, 1.0):
            tt(t2, ee, pq, ALU.mult)
            ts(pq, t2, -1.0, ALU.mult, cst, ALU.add)
        tt(t2, ee, pq, ALU.mult)
        o0 = LC[:, :, :, 0:1]
        o1 = LC[:, :, :, 1:2]
        ts(o0.w(o0.ap.rearrange("p l c o -> p (l c o)")), t2, -8.0, ALU.mult)
        ts(o1.w(o1.ap.rearrange("p l c o -> p (l c o)")), t2, -16.0, ALU.mult)

    def norm_sq(t4):
        c0, c1 = t4 * 512, (t4 + 1) * 512
        for dc in range(8):
            tt(XSQ[dc][:, :], X[:, dc, c0:c1], X[:, dc, c0:c1], ALU.mult, eng="pool")

    def norm_rest(l, s, t4, final=False):
        c0, c1 = t4 * 512, (t4 + 1) * 512
        pst = PS(6)
        prs = PS(7)
        for dc in range(8):
            mm(pst[:, :], onesb, XSQ[dc][:, :], dc == 0, dc == 7)
        act(LNT[:, :], pst[:, :], AF.Ln, bias=cp("eps"), scale=1.0 / D)
        act(prs[:, :], LNT[:, :], AF.Exp, scale=-0.5)
        for dc in range(8):
            if final:
                o = TMPN[dc % 2]
                stt(o[:, :], X[:, dc, c0:c1], cp("gfin", dc, dc + 1), prs[:, :], ALU.mult, ALU.mult)
                dma("sp", P.dv("outT", DR("outT", [D, T], F32, "ExternalOutput")[dc * 128:(dc + 1) * 128, c0:c1]), o[:, :], f"ost{dc % 2}")
            else:
                tt(TMPN[dc % 2][:, :], X[:, dc, c0:c1], prs[:, :], ALU.mult)
                act(H[:, dc, c0:c1], TMPN[dc % 2][:, :], AF.Identity,
                    bias=MT[:, l, 3 * s * 8 + dc:3 * s * 8 + dc + 1], scale=AM[:, l, s, dc:dc + 1])

    def norm_tile(l, s, t4, final=False):
        norm_sq(t4)
        norm_rest(l, s, t4, final)

    def ffn(l, s, pre_normed=False, next_norm=None):
        fi = 0 if s == 0 else 1
        issued = set()

        def load_wgu(j):
            if j in issued or j >= NJ:
                return
            issued.add(j)
            dma("pool", WGU[j % 3][:, :, :], P.dv("wgu", d_wgu(fi)[l, j], False), f"wgu{j % 3}")

        def load_wd(j0, j1):
            for jj in range(j1 - j0):
                j = j0 + jj
                dma("pool", WD[:, jj, :], P.dv("wdn", d_wdn(fi)[l, j * 128:(j + 1) * 128, :], False), f"wd{jj}")

        for j in range(3):
            load_wgu(j)
        load_wd(*GROUPS[0])
        if not pre_normed:
            for t4 in range(NTT):
                norm_tile(l, s, t4)
        cnt = 0
        for gi, (j0, j1) in enumerate(GROUPS):
            G = j1 - j0
            if gi > 0:
                load_wd(j0, j1)
            for jj in range(G):
                j = j0 + jj
                load_wgu(j)
                wt = WGU[j % 3]
                for t4 in range(NTT):
                    c0, c1 = t4 * 512, (t4 + 1) * 512
                    pg = PS((cnt % 2) * 2)
                    pu = PS((cnt % 2) * 2 + 1)
                    for kc in range(8):
                        mm(pg[:, :], wt[:, kc, 0:128], H[:, kc, c0:c1], kc == 0, kc == 7)
                    for kc in range(8):
                        mm(pu[:, :], wt[:, kc, 128:256], H[:, kc, c0:c1], kc == 0, kc == 7)
                    sg = SG[cnt % 2]
                    act(sg[:, :], pg[:, :], AF.Silu)
                    tt(ACTB[:, jj, c0:c1], pu[:, :], sg[:, :], ALU.mult)
                    cnt += 1
            for j in range(j1, j1 + 3):
                load_wgu(j)
            oc = 0
            for t4 in range(NTT):
                c0, c1 = t4 * 512, (t4 + 1) * 512
                for dc in range(8):
                    po = PS(4 + oc % 2)
                    oc += 1
                    for jj in range(G):
                        mm(po[:, :], WD[:, jj, dc * 128:(dc + 1) * 128], ACTB[:, jj, c0:c1], jj == 0, jj == G - 1)
                    stt(X[:, dc, c0:c1], po[:, :], GT[:, l, s, dc:dc + 1], X[:, dc, c0:c1], ALU.mult, ALU.add)
                if next_norm is not None and gi == len(GROUPS) - 1:
                    if t4 >= 1:
                        norm_rest(next_norm[0], next_norm[1], t4 - 1, next_norm[2])
                    norm_sq(t4)
            if next_norm is not None and gi == len(GROUPS) - 1:
                norm_rest(next_norm[0], next_norm[1], NTT - 1, next_norm[2])

    def load_win(l, ch, i):
        dma("pool", WIN[i % 3][:, :, :], P.dv("win", d_win()[l, ch], False), f"win{i % 3}")
        return WIN[i % 3]

    wi = [0]
    pc = [0]

    def nps():
        b = pc[0] % 4
        pc[0] += 1
        return PS(b)

    def mix_a(l, pre_normed=False):
        def nextw(ch):
            w = load_win(l, ch, wi[0])
            wi[0] += 1
            return w

        wpre = [nextw(CH_K[0]), nextw(CH_K[1]), nextw(CH_V)]
        if not pre_normed:
            for t4 in range(NTT):
                norm_tile(l, 1, t4)
        for c in range(2):
            w = wpre[c]
            pb = nps()
            for kc in range(8):
                mm(pb[:, 0:128], w[:, kc, :], H[:, kc, T - 128:T], kc == 0, kc == 7)
            act(HST[:, c * 128:(c + 1) * 128], pb[:, 0:128], AF.Copy)
        w = wpre[2]
        pb = nps()
        for kc in range(8):
            mm(pb[:, 0:128], H[:, kc, T - 128:T], w[:, kc, :], kc == 0, kc == 7)
        act(HST[:, 256:384], pb[:, 0:128], AF.Copy)
        for c in range(2):
            w = nextw(CH_LX[c])
            pb = nps()
            for kc in range(8):
                mm(pb[:, 0:4], w[:, kc, :], H[:, kc, T - 4:T], kc == 0, kc == 7)
            act(HST[:, 384 + 3 * c:387 + 3 * c], pb[:, 1:4], AF.Copy)
        for c in range(2):
            w = nextw(CH_SC[c])
            pb = nps()
            for kc in range(8):
                mm(pb[:, 0:2], w[:, kc, :], H[:, kc, T - 2:T], kc == 0, kc == 7)
            act(SMALL[:, 60:62], pb[:, 0:2], AF.Copy)
            w = nextw(CH_SX[c])
            pb2 = nps()
            for kc in range(8):
                mm(pb2[:, 0:2], w[:, kc, :], H[:, kc, T - 2:T], kc == 0, kc == 7)
            tt(HST[:, 390 + 2 * c:392 + 2 * c], pb2[:, 0:2], SMALL[:, 60:62], ALU.mult)
        P.op("dve", lambda e: e.memset(HST[:, 394:400].ap, 0.0), writes=[HST[:, 394:400]])
        if fused:
            d1 = DR(f"e1in{l}", [128, 400], F32)
            DR(f"e1out{l}", [128 * NG, 400], F32)
            dma("sp", P.dv(f"e1in{l}", d1[:, :]), HST[:, :], "e1st")
            collective(f"e1in{l}", f"e1out{l}", "cc1")
        else:
            d1 = DR("o_e1", [128, 400], F32, "ExternalOutput")
            out_tokens.append(dma("sp", P.dv("o_e1", d1[:, :]), HST[:, :], "e1st"))

    def mix_b(l):
        def nextw(ch):
            w = load_win(l, ch, wi[0])
            wi[0] += 1
            return w

        if fused:
            d1o = DR(f"e1out{l}", [128 * NG, 400], F32)
            dma("sp", HALL[:, :, :], P.dv(f"e1out{l}", d1o.ap().rearrange("(r p) f -> p r f", p=128)), "e1ld")
        else:
            for t4 in range(NTT):
                norm_tile(l, 1, t4)
            d1o = DR("i_e1all", [128 * NG, 400], F32, "ExternalInput")
            dma("sp", HALL[:, :, :], P.dv("i_e1all", d1o.ap().rearrange("(r p) f -> p r f", p=128), False), "e1ld")

        for c in range(4):
            w = nextw(CH_Q[c])
            for t4 in range(NTT):
                c0, c1 = t4 * 512, (t4 + 1) * 512
                pb = nps()
                for kc in range(8):
                    mm(pb[:, :], w[:, kc, :], H[:, kc, c0:c1], kc == 0, kc == 7)
                act(QT[:, c, c0:c1], pb[:, :], AF.Copy)
        for c in range(2):
            w = nextw(CH_K[c])
            for t4 in range(NTT):
                c0, c1 = t4 * 512, (t4 + 1) * 512
                pb = nps()
                for kc in range(8):
                    mm(pb[:, :], w[:, kc, :], H[:, kc, c0:c1], kc == 0, kc == 7)
                cpy(KT[:, c, 128 + c0:128 + c1], pb[:, :])
        w = nextw(CH_V)
        for t4 in range(NTT):
            pb = nps()
            for q in range(4):
                blk = t4 * 4 + q
                for kc in range(8):
                    mm(pb[:, q * 128:(q + 1) * 128], H[:, kc, blk * 128:(blk + 1) * 128], w[:, kc, :], kc == 0, kc == 7)
            o = VV[:, 1 + t4 * 4:5 + t4 * 4, :]
            cpy(o.w(o.ap.rearrange("p a b -> p (a b)")), pb[:, :])
        ts(HSEL[:, :], HALL[:, 0, :], cp("selE1", 0, 1), ALU.mult)
        for r in range(1, NG):
            stt(HSEL[:, :], HALL[:, r, :], cp("selE1", r, r + 1), HSEL[:, :], ALU.mult, ALU.add)
        for c in range(2):
            cpy(KT[:, c, 0:128], HSEL[:, c * 128:(c + 1) * 128])
        cpy(VV[:, 0, :], HSEL[:, 256:384])

        sink = cp("sinks", l * 8, l * 8 + 8)
        ptb = [P.psum(4, [128, 8, 128], BF16), P.psum(5, [128, 8, 128], BF16)]
        PEX = [PEXP, PEXP1]

        def sm(st, k):
            return SMALL[:, 64 * st + 8 * k:64 * st + 8 * k + 8]

        def a_scores(n):
            q0, q1 = n * 128, (n + 1) * 128
            for sl in range(8):
                h = HS[sl]
                pr = (h % 2) * 64
                pb = PS(sl // 2)
                mm(pb[:, (sl % 2) * 256:(sl % 2) * 256 + 256], QT[pr:pr + 64, h // 2, q0:q1],
                   KT[pr:pr + 64, h // 4, q0:q0 + 256], True, True)

        SS = [SSB, SSB1]

        def a_softmax1(n):
            st = n % 2
            SSB = SS[st]
            MX, NMX, D8 = sm(st, 0), sm(st, 1), sm(st, 3)
            for b in range(4):
                o = SSB[:, 2 * b:2 * b + 2, :]
                bi = CBK[:, CB["bias"][0] + b * 512:CB["bias"][0] + (b + 1) * 512]
                stt(o.w(o.ap.rearrange("p a b -> p (a b)")), PS(b)[:, :], 0.125, bi, ALU.mult, ALU.add)
            if n == 0:
                o = SSB[:, :, 0:128]
                ts(o, o, cp("negbig"), ALU.add)
            P.op("dve", lambda e: e.tensor_reduce(MX.ap, SSB[:, :, :].ap, AX.X, ALU.max), reads=[SSB[:, :, :]], writes=[MX])
            tt(MX, MX, sink, ALU.max)
            ts(NMX, MX, -1.0, ALU.mult)
            tt(D8, sink, NMX, ALU.add)

        def a_exp(n):
            st = n % 2
            o = 64 * st
            for h in range(8):
                act(PEX[st][:, h, :], SS[st][:, h, :], AF.Exp, bias=SMALL[:, o + 8 + h:o + 9 + h], scale=1.0,
                    accum=SMALL[:, o + 16 + h:o + 17 + h])
            act(sm(st, 4), sm(st, 3), AF.Exp)

        def a_finish(n):
            st = n % 2
            RS, ES, DEN, RINV = sm(st, 2), sm(st, 4), sm(st, 5), sm(st, 6)
            tt(DEN, RS, ES, ALU.add)
            P.op("dve", lambda e: e.reciprocal(RINV.ap, DEN.ap), reads=[DEN], writes=[RINV])

        def b_transposes(n):
            st = n % 2
            for h in range(8):
                for hf in range(2):
                    i = h * 2 + hf
                    tr(ptb[i // 8][:, i % 8, :], PEX[st][:, h, hf * 128:(hf + 1) * 128], ident)

        def b_copies(n):
            o = PTS[:, 0:8, :]
            act(o.w(o.ap.rearrange("p a b -> p (a b)")), ptb[0][:, :, :].w(ptb[0][:, :, :].ap.rearrange("p a b -> p (a b)")), AF.Copy)
            o = PTS[:, 8:16, :]
            act(o.w(o.ap.rearrange("p a b -> p (a b)")), ptb[1][:, :, :].w(ptb[1][:, :, :].ap.rearrange("p a b -> p (a b)")), AF.Copy)

        def b_pv(n):
            st = n % 2
            RINV = sm(st, 6)
            po = PS(6)
            for h in range(8):
                for hf in range(2):
                    kvh = HS[h] // 4
                    mm(po[:, h * 64:(h + 1) * 64], PTS[:, h * 2 + hf, :], VV[:, n + hf, kvh * 64:kvh * 64 + 64],
                       hf == 0, hf == 1)
            yo = YTOK[:, :]
            rb = RINV.ap.unsqueeze(2).broadcast_to([128, 8, 64])
            P.op("dve", lambda e, yo=yo, po=po, rb=rb: e.tensor_tensor(
                yo.ap.rearrange("p (h d) -> p h d", h=8), po[:, :].ap.rearrange("p (h d) -> p h d", h=8), rb, ALU.mult),
                reads=[po[:, :], RINV], writes=[yo])

        def b_out(n):
            q0, q1 = n * 128, (n + 1) * 128
            pyt = P.psum(7, [128, 4, 128], BF16)
            for c in range(4):
                tr(pyt[:, c, :], YTOK[:, c * 128:(c + 1) * 128], ident)
            act(QT[:, :, q0:q1], pyt[:, :, :], AF.Copy)

        a_scores(0)
        a_softmax1(0)
        a_exp(0)
        for n in range(16):
            nxt = n + 1 < 16
            if nxt:
                a_scores(n + 1)
            b_transposes(n)
            b_copies(n)
            if nxt:
                a_softmax1(n + 1)
            a_finish(n)
            b_pv(n)
            if nxt:
                a_exp(n + 1)
            b_out(n)

        P.op("dve", lambda e: e.memset(ZERO[:, :].ap, 0.0), writes=[ZERO[:, :]])
        gbd = cb("gbd")

        def gb(ax, c):
            o = ((l * 2 + ax) * 2 + c) * 128
            return CBK[:, CB["gbd"][0] + o:CB["gbd"][0] + o + 128]

        lp = [0]

        def nps6():
            b = lp[0] % 6
            lp[0] += 1
            return PS(b)

        for c in range(2):
            w = nextw(CH_LX[c])
            cpy(LXB[:, 0:3], HSEL[:, 384 + 3 * c:387 + 3 * c])
            for t4 in range(NTT):
                c0, c1 = t4 * 512, (t4 + 1) * 512
                pb = nps6()
                for kc in range(8):
                    mm(pb[:, :], w[:, kc, :], H[:, kc, c0:c1], kc == 0, kc == 7)
                act(LXB[:, 3 + c0:3 + c1], pb[:, :], AF.Copy)
            wg = nextw(CH_LG[c])
            cw = lambda k, c=c: cp("convw", (l * 2 + c) * 4 + k, (l * 2 + c) * 4 + k + 1)

            def s1(t4, c=c, wg=wg, cw=cw):
                c0, c1 = t4 * 512, (t4 + 1) * 512
                lxc = LXC[t4 % 2]
                ts(lxc[:, :], LXB[:, c0:c1], cw(0), ALU.mult, cp("convb", l * 2 + c, l * 2 + c + 1), ALU.add)
                for k in range(1, 4):
                    stt(lxc[:, :], LXB[:, c0 + k:c1 + k], cw(k), lxc[:, :], ALU.mult, ALU.add)
                lb = LXCB[t4 % 2]
                act(lb[:, :], lxc[:, :], AF.Copy)
                pr_ = nps6()
                mm(pr_[:, :], gb(0, c), lb[:, :], True, True)
                pi_ = nps6()
                mm(pi_[:, :], gb(1, c), lb[:, :], True, True)
                pg_ = nps6()
                for kc in range(8):
                    mm(pg_[:, :], wg[:, kc, :], H[:, kc, c0:c1], kc == 0, kc == 7)
                return pr_, pi_, pg_

            def s2(t4, pr_, pi_, pg_, c=c):
                c0, c1 = t4 * 512, (t4 + 1) * 512
                lxc = LXC[t4 % 2]
                act(RR[:, :], pr_[:, :], AF.Sigmoid, bias=cp("ba", l * 2 + c, l * 2 + c + 1), scale=1.0)
                act(II[:, :], pi_[:, :], AF.Sigmoid, bias=cp("bx", l * 2 + c, l * 2 + c + 1), scale=1.0)
                act(AA[:, :], RR[:, :], AF.Exp, scale=LC[:, l, c, 0:1])
                act(A2[:, :], RR[:, :], AF.Exp, scale=LC[:, l, c, 1:2])
                act(MM[:, :], A2[:, :], AF.Sqrt, bias=cp("one"), scale=-1.0)
                if t4 == 0:
                    ts(MM[:, 0:1], MM[:, 0:1], cp("omf"), ALU.mult, cp("f"), ALU.add)
                tt(UU[:, :], II[:, :], lxc[:, :], ALU.mult)
                tt(UU[:, :], UU[:, :], MM[:, :], ALU.mult)
                hl = HL[t4 % 2]
                pp = PP[t4 % 2]
                if t4 == 0:
                    P.op("dve", lambda e, hl=hl: e.tensor_tensor_scan(hl[:, :].ap, AA[:, :].ap, UU[:, :].ap, 0.0, ALU.mult, ALU.add),
                         reads=[AA[:, :], UU[:, :]], writes=[hl[:, :]])
                    P.op("dve", lambda e, pp=pp: e.tensor_tensor_scan(pp[:, :].ap, AA[:, :].ap, ZERO[:, :].ap, 1.0, ALU.mult, ALU.add),
                         reads=[AA[:, :], ZERO[:, :]], writes=[pp[:, :]])
                else:
                    hp = HL[(t4 - 1) % 2][:, 511:512]
                    ppv = PP[(t4 - 1) % 2][:, 511:512]
                    P.op("dve", lambda e, hl=hl, hp=hp: e.tensor_tensor_scan(hl[:, :].ap, AA[:, :].ap, UU[:, :].ap, hp.ap, ALU.mult, ALU.add),
                         reads=[AA[:, :], UU[:, :], hp], writes=[hl[:, :]])
                    P.op("dve", lambda e, pp=pp, ppv=ppv: e.tensor_tensor_scan(pp[:, :].ap, AA[:, :].ap, ZERO[:, :].ap, ppv.ap, ALU.mult, ALU.add),
                         reads=[AA[:, :], ZERO[:, :], ppv], writes=[pp[:, :]])
                act(GG[:, :], pg_[:, :], AF.Gelu_apprx_tanh)
                tt(YL[:, c, c0:c1], GG[:, :], hl[:, :], ALU.mult)
                tt(GP[:, c, c0:c1], GG[:, :], pp[:, :], ALU.mult)

            cur = s1(0)
            for t4 in range(NTT):
                nxt = s1(t4 + 1) if t4 + 1 < NTT else None
                s2(t4, *cur)
                cur = nxt
            if c == 0:
                P.op("dve", lambda e: e.memset(E2ST[:, :].ap, 0.0), writes=[E2ST[:, :]])
            cpy(E2ST[:, c:c + 1], HL[(NTT - 1) % 2][:, 511:512])
            cpy(E2ST[:, 2 + c:3 + c], PP[(NTT - 1) % 2][:, 511:512])
        if fused:
            d2 = DR(f"e2in{l}", [128, 8], F32)
            d2o = DR(f"e2out{l}", [128 * NG, 8], F32)
            dma("sp", P.dv(f"e2in{l}", d2[:, :]), E2ST[:, :], "e2st")
            collective(f"e2in{l}", f"e2out{l}", "cc2")
            dma("sp", E2ALL[:, :, :], P.dv(f"e2out{l}", d2o.ap().rearrange("(r p) f -> p r f", p=128)), "e2ld")
        else:
            d2 = DR("o_e2", [128, 8], F32, "ExternalOutput")
            out_tokens.append(dma("sp", P.dv("o_e2", d2[:, :]), E2ST[:, :], "e2st"))

        for c in range(2):
            wc = nextw(CH_SC[c])
            cpy(ZB[:, 0:2], HSEL[:, 390 + 2 * c:392 + 2 * c])
            for t4 in range(NTT):
                c0, c1 = t4 * 512, (t4 + 1) * 512
                pb = nps()
                for kc in range(8):
                    mm(pb[:, :], wc[:, kc, :], H[:, kc, c0:c1], kc == 0, kc == 7)
                act(ZB[:, 2 + c0:2 + c1], pb[:, :], AF.Copy)
            wx = nextw(CH_SX[c])
            for t4 in range(NTT):
                c0, c1 = t4 * 512, (t4 + 1) * 512
                pb = nps()
                for kc in range(8):
                    mm(pb[:, :], wx[:, kc, :], H[:, kc, c0:c1], kc == 0, kc == 7)
                tt(ZB[:, 2 + c0:2 + c1], pb[:, :], ZB[:, 2 + c0:2 + c1], ALU.mult)
            wb = nextw(CH_SB[c])
            for t4 in range(NTT):
                c0, c1 = t4 * 512, (t4 + 1) * 512
                zc = ZC[t4 % 2]
                sw = lambda k: cp("scw", (l * 2 + c) * 3 + k, (l * 2 + c) * 3 + k + 1)
                ts(zc[:, :], ZB[:, c0:c1], sw(0), ALU.mult)
                for k in range(1, 3):
                    stt(zc[:, :], ZB[:, c0 + k:c1 + k], sw(k), zc[:, :], ALU.mult, ALU.add)
                pb = nps()
                for kc in range(8):
                    mm(pb[:, :], wb[:, kc, :], H[:, kc, c0:c1], kc == 0, kc == 7)
                tt(YS[:, c, c0:c1], pb[:, :], zc[:, :], ALU.mult)

        if not fused:
            dy = DR("o_ystate", [128, 10, T], F32, "ExternalOutput")
            for i, (buf, c) in enumerate([(QT, 0), (QT, 1), (QT, 2), (QT, 3), (YL, 0), (YL, 1), (GP, 0), (GP, 1), (YS, 0), (YS, 1)]):
                out_tokens.append(dma("pool", P.dv("o_ystate", dy[:, i, :]), buf[:, c, :], f"yst{i}"))

    def mix_c(l):
        if not fused:
            dy = DR("i_ystate", [128, 10, T], F32, "ExternalInput")
            for i, (buf, c) in enumerate([(QT, 0), (QT, 1), (QT, 2), (QT, 3), (YL, 0), (YL, 1), (GP, 0), (GP, 1), (YS, 0), (YS, 1)]):
                dma("pool", buf[:, c, :], P.dv("i_ystate", dy[:, i, :], False), f"yld{i}")
            d2o = DR("i_e2all", [128 * NG, 8], F32, "ExternalInput")
            dma("sp", E2ALL[:, :, :], P.dv("i_e2all", d2o.ap().rearrange("(r p) f -> p r f", p=128), False), "e2ld")
        sel = cp("selE2")
        oms = cp("omselE2")
        P.op("dve", lambda e: e.tensor_tensor(E2PM[:, :, :].ap, E2ALL[:, :, 2:4].ap, sel.ap.rearrange("p (a b) -> p a b", b=2), ALU.mult),
             reads=[E2ALL[:, :, 2:4], sel], writes=[E2PM[:, :, :]])
        P.op("dve", lambda e: e.tensor_tensor(E2PM[:, :, :].ap, E2PM[:, :, :].ap, oms.ap.rearrange("p (a b) -> p a b", b=2), ALU.add),
             reads=[E2PM[:, :, :], oms], writes=[E2PM[:, :, :]])
        P.op("dve", lambda e: e.tensor_tensor(E2HM[:, :, :].ap, E2ALL[:, :, 0:2].ap, sel.ap.rearrange("p (a b) -> p a b", b=2), ALU.mult),
             reads=[E2ALL[:, :, 0:2], sel], writes=[E2HM[:, :, :]])
        P.op("dve", lambda e: e.memset(HIN[:, :].ap, 0.0), writes=[HIN[:, :]])
        for r in range(NG):
            tt(HIN[:, :], HIN[:, :], E2PM[:, r, :], ALU.mult)
            tt(HIN[:, :], HIN[:, :], E2HM[:, r, :], ALU.add)
        for c in range(2):
            for t4 in range(NTT):
                c0, c1 = t4 * 512, (t4 + 1) * 512
                stt(YL[:, c, c0:c1], GP[:, c, c0:c1], HIN[:, c:c + 1], YL[:, c, c0:c1], ALU.mult, ALU.add)

        oc = 0
        for dc in range(8):
            wo = WOUT[dc % 2]
            dma("pool", wo[:, :, :], P.dv("wout", d_wout()[l, dc], False), f"wout{dc % 2}")
            for t4 in range(NTT):
                c0, c1 = t4 * 512, (t4 + 1) * 512
                po = PS(4 + oc % 2)
                oc += 1
                for kc in range(8):
                    if kc < 4:
                        rhs = QT[:, kc, c0:c1]
                    elif kc < 6:
                        rhs = YL[:, kc - 4, c0:c1]
                    else:
                        rhs = YS[:, kc - 6, c0:c1]
                    mm(po[:, :], wo[:, kc, :], rhs, kc == 0, kc == 7)
                stt(X[:, dc, c0:c1], po[:, :], GT[:, l, 1, dc:dc + 1], X[:, dc, c0:c1], ALU.mult, ALU.add)

    pro_common()
    only_pro = (list(stages) == [("pro",)])
    if fused or only_pro:
        pro_modparts()
    if not only_pro:
        pro_modfinish()
    pre = False
    fin_done = False
    for si, st in enumerate(stages):
        if st[0] == "ffn":
            nn = None
            if fused and si + 1 < len(stages):
                nx = stages[si + 1]
                if nx[0] == "ffn":
                    nn = (nx[1], nx[2], False)
                elif nx[0] == "mix":
                    nn = (nx[1], 1, False)
                elif nx[0] == "fin":
                    nn = (0, 0, True)
                    fin_done = True
            ffn(st[1], st[2], pre_normed=pre, next_norm=nn)
            pre = nn is not None
        elif st[0] == "mix":
            mix_a(st[1], pre_normed=pre)
            pre = False
            mix_b(st[1])
            mix_c(st[1])
        elif st[0] == "ma":
            mix_a(st[1])
        elif st[0] == "mb":
            mix_b(st[1])
        elif st[0] == "mc":
            mix_c(st[1])
    d_out = None
    if ("fin",) in stages:
        d_out = DR("outT", [D, T], F32, "ExternalOutput")
        if not fin_done:
            for t4 in range(NTT):
                norm_tile(0, 0, t4, final=True)
    elif not only_pro and stages[-1][0] != "mb":
        d_out = DR("outT", [D, T], F32, "ExternalOutput")
        for dc in range(8):
            dma("sp", P.dv("outT", d_out[dc * 128:(dc + 1) * 128, :]), X[:, dc, :], f"xst{dc}")
    finals = list(out_tokens)
    for sname, v in P.slot_val.items():
        if sname.startswith("ost") or sname.startswith("xst"):
            finals.append(("d", sname, v))
    P.finalize(finals)
    return P


def _alibi_table():
    slopes = (2.0 ** (-8.0 * np.arange(1, 9) / 8)).astype(np.float32)
    qi = np.arange(128)[:, None]
    kj = np.arange(256)[None, :]
    dist = qi + 128 - kj
    valid = (dist >= 0) & (dist < 128)
    tab = np.where(valid[None], -slopes[:, None, None] * dist[None].astype(np.float32), np.float32(-1e30))
    tab = tab[HS]
    return np.ascontiguousarray(tab.transpose(1, 0, 2)).astype(np.float32)


def prepare_inputs(inp):
    f = np.float32
    x = np.asarray(inp["x"], f)
    c = np.asarray(inp["c"], f)
    w_mod = np.asarray(inp["w_mod"], f)
    b_mod = np.asarray(inp["b_mod"], f)
    shared = {}
    for s, name in enumerate(["w_ffn1_gu", "w_ffn2_gu"]):
        w = np.asarray(inp[name], f)
        g = w[:, :, :DFF].reshape(DEPTH, 8, 128, NJ, 128)
        u = w[:, :, DFF:].reshape(DEPTH, 8, 128, NJ, 128)
        gu = np.stack([g, u], axis=4)
        shared[f"wgu{s}"] = np.ascontiguousarray(gu.transpose(0, 3, 2, 1, 4, 5)).reshape(DEPTH, NJ, 128, 8, 256)
    shared["wdn0"] = np.ascontiguousarray(np.asarray(inp["w_ffn1_down"], f))
    shared["wdn1"] = np.ascontiguousarray(np.asarray(inp["w_ffn2_down"], f))
    w_in = np.asarray(inp["w_in"], f)
    cols = []
    for cq in range(4):
        cols.append(np.arange(cq * 128, (cq + 1) * 128))
    cols.append(np.concatenate([np.arange(512, 576), np.arange(512, 576)]))
    cols.append(np.concatenate([np.arange(576, 640), np.arange(576, 640)]))
    cols.append(np.arange(640, 768))
    for base in (768, 1024, 1280, 1536, 1792):
        cols.append(np.arange(base, base + 128))
        cols.append(np.arange(base + 128, base + 256))
    colidx = np.stack(cols)
    wi = w_in[:, :, colidx]
    wi = wi.reshape(DEPTH, 8, 128, NCH_IN, 128).transpose(0, 3, 2, 1, 4)
    shared["win"] = np.ascontiguousarray(wi)
    wo = np.asarray(inp["w_out"], f).copy()
    attn_rows = np.concatenate([np.arange(HS[sl] * 64, HS[sl] * 64 + 64) for sl in range(8)])
    wo[:, :512, :] = wo[:, attn_rows, :]
    wo = wo.reshape(DEPTH, 8, 128, 8, 128).transpose(0, 3, 2, 1, 4)
    shared["wout"] = np.ascontiguousarray(wo)

    cbk = np.zeros((128, NCB), f)
    cbk[:, CB["ident"][0]:CB["ident"][1]] = np.eye(128, dtype=f)
    cbk[:, CB["ones"][0]:CB["ones"][1]] = 1.0
    ga = np.asarray(inp["lru_gate_a_w"], f)
    gx = np.asarray(inp["lru_gate_x_w"], f)
    gbd = np.zeros((128, DEPTH, 2, 2, 128), f)
    for l in range(DEPTH):
        for ax, gw in enumerate((ga, gx)):
            for cc in range(2):
                for hh in range(2):
                    gbd[hh * 64:(hh + 1) * 64, l, ax, cc, hh * 64:(hh + 1) * 64] = gw[l, cc * 2 + hh]
    cbk[:, CB["gbd"][0]:CB["gbd"][1]] = gbd.reshape(128, -1)
    cbk[:, CB["bias"][0]:CB["bias"][1]] = _alibi_table().reshape(128, -1)
    shared["cpackb"] = cbk

    def colT(a):
        a = np.asarray(a, f)
        lead = a.shape[:-1]
        return np.moveaxis(a.reshape(*lead, 2, 128), -1, 0)

    in_maps = []
    for core in range(NCORES):
        b, ch = core // 4, core % 4
        m = dict(shared)
        m["xT"] = np.ascontiguousarray(x[b, ch * T:(ch + 1) * T, :].T)
        cpk = np.zeros((128, NCP), f)

        def put(name, arr):
            a, bb = CP[name]
            cpk[:, a:bb] = np.asarray(arr, f).reshape(128, bb - a)

        put("cT", c[b].reshape(8, 128).T)
        put("gn", np.asarray(inp["g_norm"], f).reshape(DEPTH, 3, 8, 128).transpose(3, 0, 1, 2))
        put("gfin", np.asarray(inp["g_final"], f).reshape(8, 128).T)
        put("convw", colT(np.asarray(inp["lru_conv_w"], f)).transpose(0, 1, 3, 2))
        put("convb", colT(inp["lru_conv_b"]))
        put("ba", colT(inp["lru_gate_a_b"]))
        put("bx", colT(inp["lru_gate_x_b"]))
        put("lam", colT(inp["lru_lambda"]))
        put("scw", colT(np.asarray(inp["sc_conv_w"], f)).transpose(0, 1, 3, 2))
        put("sinks", np.broadcast_to(np.asarray(inp["attn_sinks"], f)[:, HS].reshape(1, 16), (128, 16)))
        se1 = np.zeros(NG, f)
        if ch > 0:
            se1[ch - 1] = 1.0
        put("selE1", np.broadcast_to(se1, (128, NG)))
        se2 = np.zeros(NG, f)
        se2[:ch] = 1.0
        put("selE2", np.broadcast_to(np.repeat(se2, 2), (128, 2 * NG)))
        put("omselE2", np.broadcast_to(np.repeat(1.0 - se2, 2), (128, 2 * NG)))
        first = 1.0 if ch == 0 else 0.0
        put("f", np.full((128, 1), first, f))
        put("omf", np.full((128, 1), 1.0 - first, f))
        put("negbig", np.full((128, 1), -1e30 if ch == 0 else 0.0, f))
        put("eps", np.full((128, 1), 1e-6, f))
        put("one", np.ones((128, 1), f))
        put("zero", np.zeros((128, 1), f))
        put("identf", np.eye(128, dtype=f))
        m["cpack"] = cpk
        wm = w_mod[:, :, ch * 2304:(ch + 1) * 2304].reshape(DEPTH, 8, 128, 2304).transpose(0, 2, 1, 3)
        m["wmod"] = np.ascontiguousarray(wm)
        m["bmod"] = np.ascontiguousarray(b_mod[None, :, ch * 2304:(ch + 1) * 2304])
        in_maps.append(m)
    return in_maps


FULL_STAGES = [("ffn", 0, 0), ("mix", 0), ("ffn", 0, 2), ("ffn", 1, 0), ("mix", 1), ("ffn", 1, 2), ("fin",)]
LAUNCHES = [
    [("pro",)],
    [("ffn", 0, 0), ("ma", 0)],
    [("mb", 0)],
    [("mc", 0), ("ffn", 0, 2), ("ffn", 1, 0), ("ma", 1)],
    [("mb", 1)],
    [("mc", 1), ("ffn", 1, 2), ("fin",)],
]
USE_FUSED = True
_PROG_CACHE = {}


def _get_prog(stages, fused):
    key = (tuple(stages), fused)
    if key not in _PROG_CACHE:
        _PROG_CACHE[key] = build_program(list(stages), fused=fused)
    return _PROG_CACHE[key]


def _launch(stages, fused, base, extra, trace=False):
    P = _get_prog(stages, fused)
    in_maps = []
    for c in range(NCORES):
        m = {}
        for k in P.ext_in:
            m[k] = extra[c][k] if k in extra[c] else base[c][k]
        in_maps.append(m)
    res = run_bass_kernel_spmd(P.nc, in_maps, core_ids=list(range(NCORES)), trace=trace)
    return res


def _assemble(results):
    out = np.empty((2, 4 * T, D), np.float32)
    for core in range(NCORES):
        b, ch = core // 4, core % 4
        out[b, ch * T:(ch + 1) * T, :] = np.asarray(results[core]["outT"]).reshape(D, T).T
    return out


def run_stages(inp, stages, trace=False, fused=True):
    base = prepare_inputs(inp)
    res = _launch(stages, fused, base, [{} for _ in range(NCORES)], trace=trace)
    return _assemble(res.results), res


def run_unfused(inp, upto=None):
    base = prepare_inputs(inp)
    cores = range(NCORES)
    extra = [{} for _ in cores]
    r = _launch(LAUNCHES[0], False, base, extra).results

    def gather(key, shape):
        for g in range(NCORES // NG):
            grp = range(g * NG, (g + 1) * NG)
            yield grp, np.ascontiguousarray(np.concatenate([np.asarray(key(c)).reshape(shape) for c in grp], axis=0))

    for grp, arr in gather(lambda c: r[c]["o_modpart"], (36, 128)):
        for c in grp:
            extra[c]["i_modall"] = arr
    res = None
    for li, st in enumerate(LAUNCHES[1:]):
        if upto is not None and li >= upto:
            break
        res = _launch(st, False, base, extra).results
        if "outT" in res[0]:
            for c in cores:
                extra[c]["xT"] = np.ascontiguousarray(res[c]["outT"]).reshape(D, T)
        if "o_e1" in res[0]:
            for grp, arr in gather(lambda c: res[c]["o_e1"], (128, 400)):
                for c in grp:
                    extra[c]["i_e1all"] = arr
        if "o_e2" in res[0]:
            for grp, arr in gather(lambda c: res[c]["o_e2"], (128, 8)):
                for c in grp:
                    extra[c]["i_e2all"] = arr
            for c in cores:
                extra[c]["i_ystate"] = np.ascontiguousarray(res[c]["o_ystate"]).reshape(128, 10, T)
    return res, extra


def kernel(**inputs):
    if USE_FUSED:
        out, _ = run_stages(inputs, FULL_STAGES, fused=True)
        return out
    res, _ = run_unfused(inputs)
    return _assemble(res)
```
